# Optimizing a Trainium2 kernel written in Bass

```python
import math
import jax
import jax.numpy as jnp
from jax import lax
import numpy as np

D_MODEL = 1024
BATCH = 4
SEQ = 4096
DEPTH = 4
DEC_BATCH = 128
DEC_SEQ = 1
PAST_LEN = 8192
PAGE_SIZE = 128

N_A = DEPTH // 2
N_B = DEPTH - N_A
MEM_TOKENS = 256
MEM_HEADS = 4
MEM_W = D_MODEL // 4
MEM_HD = MEM_W // MEM_HEADS
MIX_MAIN = D_MODEL - MEM_W
RET_HEADS = 6
RET_HD = MIX_MAIN // RET_HEADS
RET_CHUNK = 128
SWA_HEADS = 12
SWA_KV_HEADS = 4
SWA_HD = MIX_MAIN // SWA_HEADS
WINDOW = 128
D_FF = 4 * D_MODEL
EPS = 1e-6

kernel_name = 'yoco_retention_swa_sink_memory_decoder'

F32 = jnp.float32


def rmsnorm(x, g):
    xf = x.astype(F32)
    y = xf * lax.rsqrt(jnp.mean(xf * xf, axis=-1, keepdims=True) + EPS)
    return (y * g.astype(F32)).astype(x.dtype)


def sq_relu_mlp(x, g, w_up, w_down):
    h = rmsnorm(x, g) @ w_up
    return jnp.square(jax.nn.relu(h)) @ w_down


def head_groupnorm(o):
    mu = jnp.mean(o, axis=-1, keepdims=True)
    var = jnp.mean(jnp.square(o - mu), axis=-1, keepdims=True)
    return (o - mu) * lax.rsqrt(var + EPS)


def retention_log_decay():
    return jnp.asarray(np.log1p(-(2.0 ** (-5.0 - np.arange(RET_HEADS)))), F32)


def alibi_slopes(n):
    def pow2(m):
        start = 2.0 ** (-8.0 / m)
        return [start ** (i + 1) for i in range(m)]
    if math.log2(n).is_integer():
        s = pow2(n)
    else:
        c = 2 ** int(math.floor(math.log2(n)))
        s = pow2(c) + pow2(2 * c)[0::2][: n - c]
    return np.asarray(s, np.float32)


def retention(q, k, v, s0):
    b, L, h, dk = q.shape
    dv = v.shape[-1]
    c = math.gcd(L, RET_CHUNK)
    n = L // c
    log_g = retention_log_decay()
    qf = q.astype(F32).reshape(b, n, c, h, dk)
    kf = (k.astype(F32) * dk ** -0.5).reshape(b, n, c, h, dk)
    vf = v.astype(F32).reshape(b, n, c, h, dv)
    idx = jnp.arange(c, dtype=F32)
    diff = idx[:, None] - idx[None, :]
    decay = jnp.where(diff >= 0, jnp.exp(jnp.maximum(diff, 0.0)[None] * log_g[:, None, None]), 0.0)
    scores = jnp.einsum('bnqhd,bnkhd->bnhqk', qf, kf) * decay
    inner = jnp.einsum('bnhqk,bnkhe->bnqhe', scores, vf)
    w_k = jnp.exp((c - 1.0 - idx)[:, None] * log_g[None, :])
    kv_chunk = jnp.einsum('bnkhd,bnkhe->bnhde', kf * w_k[:, :, None], vf)
    g_c = jnp.exp(c * log_g)[:, None, None]

    def step(s, kvc):
        return g_c * s + kvc, s

    s_last, s_prev = lax.scan(step, s0.astype(F32), jnp.moveaxis(kv_chunk, 1, 0))
    s_prev = jnp.moveaxis(s_prev, 0, 1)
    w_q = jnp.exp((idx + 1.0)[:, None] * log_g[None, :])
    cross = jnp.einsum('bnqhd,bnhde->bnqhe', qf * w_q[:, :, None], s_prev)
    return (inner + cross).reshape(b, L, h, dv), s_last


def memory_kv(mem, g, w):
    b, m, _ = mem.shape
    k, v = jnp.split(rmsnorm(mem, g) @ w, 2, axis=-1)
    return k.reshape(b, m, MEM_HEADS, MEM_HD), v.reshape(b, m, MEM_HEADS, MEM_HD)


def memory_attention(q, mk, mv):
    s = jnp.einsum('blhd,bmhd->bhlm', q.astype(F32), mk.astype(F32)) * MEM_HD ** -0.5
    p = jax.nn.softmax(s, axis=-1)
    return jnp.einsum('bhlm,bmhd->blhd', p, mv.astype(F32)).astype(q.dtype)


def sink_softmax(s, sink):
    m = jnp.maximum(jnp.max(s, axis=-1, keepdims=True), sink)
    e = jnp.exp(s - m)
    return e / (jnp.sum(e, axis=-1, keepdims=True) + jnp.exp(sink - m))


def windowed_sink_attention(q, k, v, qpos, kpos, sinks):
    g, r, hd = q.shape[3:]
    slopes = jnp.asarray(alibi_slopes(g * r)).reshape(g, r)
    dist = qpos[:, :, None] - kpos[:, None, :]
    valid = (kpos[:, None, :] >= 0) & (dist >= 0) & (dist <= WINDOW)
    s = jnp.einsum('bnqgrd,bnkgd->bngrqk', q.astype(F32), k.astype(F32)) * hd ** -0.5
    s = s - slopes[None, None, :, :, None, None] * dist.astype(F32)[None, :, None, None]
    s = jnp.where(valid[None, :, None, None], s, -jnp.inf)
    sink = sinks.astype(F32).reshape(g, r)[None, None, :, :, None, None]
    p = sink_softmax(s, sink)
    return jnp.einsum('bngrqk,bnkgd->bnqgrd', p, v.astype(F32)).astype(q.dtype)


def window_attention_prompt(q, k, v, sinks):
    b, s_len, hq, hd = q.shape
    hkv = k.shape[2]
    nb = s_len // WINDOW
    qb = q.reshape(b, nb, WINDOW, hkv, hq // hkv, hd)
    kb = k.reshape(b, nb, WINDOW, hkv, hd)
    vb = v.reshape(b, nb, WINDOW, hkv, hd)
    shift = lambda t: jnp.concatenate([jnp.zeros_like(t[:, :1]), t[:, :-1]], axis=1)
    kk = jnp.concatenate([shift(kb), kb], axis=2)
    vv = jnp.concatenate([shift(vb), vb], axis=2)
    start = jnp.arange(nb)[:, None] * WINDOW
    qpos = start + jnp.arange(WINDOW)[None, :]
    kpos = start - WINDOW + jnp.arange(2 * WINDOW)[None, :]
    o = windowed_sink_attention(qb, kk, vv, qpos, kpos, sinks)
    return o.reshape(b, s_len, hq, hd)


def window_attention_decode(q, k_ctx, v_ctx, sinks):
    b, L, hq, hd = q.shape
    hkv = k_ctx.shape[2]
    qpos = (PAST_LEN + jnp.arange(L))[None, :]
    kpos = (PAST_LEN - WINDOW + jnp.arange(WINDOW + L))[None, :]
    o = windowed_sink_attention(q.reshape(b, 1, L, hkv, hq // hkv, hd), k_ctx[:, None], v_ctx[:, None], qpos, kpos, sinks)
    return o.reshape(b, L, hq, hd)


def retention_layer(x, mem_k, mem_v, s0, g_mix, w_in, w_out, g_mlp, w_up, w_down):
    b, L, _ = x.shape
    proj = rmsnorm(x, g_mix) @ w_in
    q, k, v, gate, q_mem = jnp.split(proj, [MIX_MAIN, 2 * MIX_MAIN, 3 * MIX_MAIN, 4 * MIX_MAIN], axis=-1)
    hs = lambda t: t.reshape(b, L, RET_HEADS, RET_HD)
    o_ret, s_new = retention(hs(q), hs(k), hs(v), s0)
    o_ret = (jax.nn.silu(gate.astype(F32)) * head_groupnorm(o_ret).reshape(b, L, MIX_MAIN)).astype(x.dtype)
    o_mem = memory_attention(q_mem.reshape(b, L, MEM_HEADS, MEM_HD), mem_k, mem_v).reshape(b, L, MEM_W)
    x = x + jnp.concatenate([o_ret, o_mem], axis=-1) @ w_out
    return x + sq_relu_mlp(x, g_mlp, w_up, w_down), s_new


def window_layer(x, k_ctx, v_ctx, decode, mem_k, mem_v, sinks, g_mix, w_in, w_out, g_mlp, w_up, w_down):
    b, L, _ = x.shape
    proj = rmsnorm(x, g_mix) @ w_in
    q, q_mem = jnp.split(proj, [MIX_MAIN], axis=-1)
    q = q.reshape(b, L, SWA_HEADS, SWA_HD)
    attn = window_attention_decode if decode else window_attention_prompt
    o_swa = attn(q, k_ctx, v_ctx, sinks).reshape(b, L, MIX_MAIN)
    o_mem = memory_attention(q_mem.reshape(b, L, MEM_HEADS, MEM_HD), mem_k, mem_v).reshape(b, L, MEM_W)
    x = x + jnp.concatenate([o_swa, o_mem], axis=-1) @ w_out
    return x + sq_relu_mlp(x, g_mlp, w_up, w_down)


def trunk(x, mem_k, mem_v, ret_s0, buf_k, buf_v, norm_mix, w_in_a, w_out_a, w_in_b, w_out_b,
          attn_sinks, norm_kv, w_kv, norm_mlp, w_up, w_down, norm_final):
    decode = buf_k is not None
    b, L, _ = x.shape
    ret_states = []
    k_ctx = v_ctx = None
    for l in range(DEPTH):
        if l < N_A:
            x, s = retention_layer(x, mem_k[l], mem_v[l], ret_s0[l], norm_mix[l], w_in_a[l], w_out_a[l],
                                   norm_mlp[l], w_up[l], w_down[l])
            ret_states.append(s)
            continue
        j = l - N_A
        if j == 0:
            k_new, v_new = jnp.split(rmsnorm(x, norm_kv) @ w_kv, 2, axis=-1)
            k_new = k_new.reshape(b, L, SWA_KV_HEADS, SWA_HD)
            v_new = v_new.reshape(b, L, SWA_KV_HEADS, SWA_HD)
            if decode:
                k_ctx = jnp.concatenate([buf_k.astype(x.dtype), k_new], axis=1)
                v_ctx = jnp.concatenate([buf_v.astype(x.dtype), v_new], axis=1)
            else:
                k_ctx, v_ctx = k_new, v_new
        x = window_layer(x, k_ctx, v_ctx, decode, mem_k[l], mem_v[l], attn_sinks[j], norm_mix[l],
                         w_in_b[j], w_out_b[j], norm_mlp[l], w_up[l], w_down[l])
    y = rmsnorm(x, norm_final)
    return y, jnp.stack(ret_states).astype(ret_s0.dtype), k_ctx[:, -WINDOW:], v_ctx[:, -WINDOW:]


def setup_inputs(seed: int = 0) -> dict:
    key = jax.random.key(seed)
    ks = jax.random.split(key, 24)
    nrm = lambda k, shape, s: s * jax.random.normal(k, shape, jnp.float32)
    gain = lambda k, shape: 1.0 + 0.02 * jax.random.normal(k, shape, jnp.float32)
    return {
        'x_prompt': nrm(ks[0], (BATCH, SEQ, D_MODEL), 1.0),
        'x_sample': nrm(ks[1], (DEC_BATCH, DEC_SEQ, D_MODEL), 1.0),
        'cache_mem_k': nrm(ks[2], (DEPTH, DEC_BATCH, MEM_TOKENS, MEM_HEADS, MEM_HD), 1.0),
        'cache_mem_v': nrm(ks[3], (DEPTH, DEC_BATCH, MEM_TOKENS, MEM_HEADS, MEM_HD), 1.0),
        'state_ret': nrm(ks[4], (N_A, DEC_BATCH, RET_HEADS, RET_HD, RET_HD), 0.3),
        'cache_swa_k': nrm(ks[5], (DEC_BATCH, WINDOW, SWA_KV_HEADS, SWA_HD), 1.0),
        'cache_swa_v': nrm(ks[6], (DEC_BATCH, WINDOW, SWA_KV_HEADS, SWA_HD), 1.0),
        'mem_prompt': nrm(ks[7], (BATCH, MEM_TOKENS, D_MODEL), 1.0),
        'norm_mix': gain(ks[8], (DEPTH, D_MODEL)),
        'w_in_a': nrm(ks[9], (N_A, D_MODEL, 4 * MIX_MAIN + MEM_W), D_MODEL ** -0.5),
        'w_out_a': nrm(ks[10], (N_A, MIX_MAIN + MEM_W, D_MODEL), (MIX_MAIN + MEM_W) ** -0.5),
        'w_in_b': nrm(ks[11], (N_B, D_MODEL, MIX_MAIN + MEM_W), D_MODEL ** -0.5),
        'w_out_b': nrm(ks[12], (N_B, MIX_MAIN + MEM_W, D_MODEL), (MIX_MAIN + MEM_W) ** -0.5),
        'attn_sinks': nrm(ks[13], (N_B, SWA_HEADS), 1.0),
        'norm_mem': gain(ks[14], (DEPTH, D_MODEL)),
        'w_mem_kv': nrm(ks[15], (DEPTH, D_MODEL, 2 * MEM_W), D_MODEL ** -0.5),
        'norm_kv': gain(ks[16], (D_MODEL,)),
        'w_kv': nrm(ks[17], (D_MODEL, 2 * SWA_KV_HEADS * SWA_HD), D_MODEL ** -0.5),
        'norm_mlp': gain(ks[18], (DEPTH, D_MODEL)),
        'w_up': nrm(ks[19], (DEPTH, D_MODEL, D_FF), D_MODEL ** -0.5),
        'w_down': nrm(ks[20], (DEPTH, D_FF, D_MODEL), D_FF ** -0.5),
        'norm_final': gain(ks[21], (D_MODEL,)),
    }


def reference(x_prompt, x_sample, cache_mem_k, cache_mem_v, state_ret, cache_swa_k, cache_swa_v, mem_prompt,
              norm_mix, w_in_a, w_out_a, w_in_b, w_out_b, attn_sinks, norm_mem, w_mem_kv, norm_kv, w_kv,
              norm_mlp, w_up, w_down, norm_final):
    mem_pairs = [memory_kv(mem_prompt, norm_mem[l], w_mem_kv[l]) for l in range(DEPTH)]
    mem_k_prompt = jnp.stack([p[0] for p in mem_pairs])
    mem_v_prompt = jnp.stack([p[1] for p in mem_pairs])
    ret0 = jnp.zeros((N_A, x_prompt.shape[0], RET_HEADS, RET_HD, RET_HD), state_ret.dtype)
    y_prompt, ret_prompt, swa_k_prompt, swa_v_prompt = trunk(
        x_prompt, mem_k_prompt, mem_v_prompt, ret0, None, None,
        norm_mix, w_in_a, w_out_a, w_in_b, w_out_b, attn_sinks, norm_kv, w_kv, norm_mlp, w_up, w_down, norm_final)
    y_sample, ret_sample, swa_k_sample, swa_v_sample = trunk(
        x_sample, cache_mem_k, cache_mem_v, state_ret, cache_swa_k, cache_swa_v,
        norm_mix, w_in_a, w_out_a, w_in_b, w_out_b, attn_sinks, norm_kv, w_kv, norm_mlp, w_up, w_down, norm_final)
    return (y_prompt, y_sample, ret_prompt, ret_sample, swa_k_prompt, swa_v_prompt, swa_k_sample, swa_v_sample,
            mem_k_prompt.astype(cache_mem_k.dtype), mem_v_prompt.astype(cache_mem_v.dtype))
```

```python
import math
import numpy as np
from contextlib import ExitStack
import ml_dtypes
import concourse.bass as bass
import concourse.mybir as mybir
from concourse.bass_utils import run_bass_kernel_spmd

F32 = mybir.dt.float32
BF16 = mybir.dt.bfloat16
ALU = mybir.AluOpType
AF = mybir.ActivationFunctionType
AX = mybir.AxisListType

D = 1024
SEQ = 4096
HALF = 2048
G = 1024
NCG = G // 128
TS = 256
CH = TS // 128
NST = G // TS
DEPTH = 4
MIX = 768
NSLOT = 17
EPS = 1e-6
RET_H = 6
GAMMA = [1.0 - 2.0 ** (-5.0 - h) for h in range(RET_H)]
LOGG = [float(np.log1p(-(2.0 ** (-5.0 - h)))) for h in range(RET_H)]
GC = [float(np.exp(128 * LOGG[h])) for h in range(RET_H)]


def alibi_slopes(n):
    def pow2(m):
        start = 2.0 ** (-8.0 / m)
        return [start ** (i + 1) for i in range(m)]
    if math.log2(n).is_integer():
        s = pow2(n)
    else:
        c = 2 ** int(math.floor(math.log2(n)))
        s = pow2(c) + pow2(2 * c)[0::2][: n - c]
    return [float(np.float32(v)) for v in s]


SLOPES = alibi_slopes(12)


class Buf:
    __slots__ = ("name", "w", "r", "const", "excl")

    def __init__(self, name, const=False, excl=False):
        self.name = name
        self.w = None
        self.r = {}
        self.const = const
        self.excl = excl


class Kern:
    def __init__(self, nc, es):
        self.nc = nc
        self.es = es
        self.eng = {"pe": nc.tensor, "act": nc.scalar, "dve": nc.vector, "pool": nc.gpsimd, "sp": nc.sync}
        self.sems = {}
        self.cnt = {}
        self.known = {e: {} for e in self.eng}
        self.cur = {}
        self.epoch = {}
        for e in self.eng:
            self.epoch[e] = 0
            self.cur[e] = f"{e}#0"
            self.newsem(self.cur[e])
        self.nops = {e: 0 for e in self.eng}

    def newsem(self, key):
        self.sems[key] = self.es.enter_context(self.nc.semaphore("s_" + key))
        self.cnt[key] = 0

    def _wait(self, e, reads, writes):
        need = {}
        for b in reads:
            if b.w is not None:
                k, c = b.w
                if need.get(k, 0) < c:
                    need[k] = c
            if b.excl:
                for k, c in b.r.items():
                    if k.split("#")[0] != e and need.get(k, 0) < c:
                        need[k] = c
        for b in writes:
            if b.w is not None:
                k, c = b.w
                if need.get(k, 0) < c:
                    need[k] = c
            for k, c in b.r.items():
                if need.get(k, 0) < c:
                    need[k] = c
        kn = self.known[e]
        for k, c in need.items():
            if kn.get(k, 0) < c:
                self.eng[e].wait_ge(self.sems[k], c)
                kn[k] = c

    def _record(self, ev, reads, writes):
        k, c = ev
        for b in reads:
            if not b.const:
                if b.r.get(k, 0) < c:
                    b.r[k] = c
        for b in writes:
            b.w = ev
            b.r = {}

    def op(self, e, fn, reads=(), writes=()):
        self._wait(e, reads, writes)
        ins = fn(self.eng[e])
        key = self.cur[e]
        if self.cnt[key] >= 2000:
            self.epoch[e] += 1
            key = f"{e}#{self.epoch[e]}"
            self.cur[e] = key
            self.newsem(key)
        self.cnt[key] += 1
        ins.then_inc(self.sems[key], 1)
        ev = (key, self.cnt[key])
        self._record(ev, reads, writes)
        self.nops[e] += 1
        return ev

    def dma(self, q, out, in_, key, reads=(), writes=(), chain=True):
        self._wait(q, reads, writes)
        if chain and self.cnt[key] > 0 and self.known[q].get(key, 0) < self.cnt[key]:
            self.eng[q].wait_ge(self.sems[key], self.cnt[key])
            self.known[q][key] = self.cnt[key]
        ins = self.eng[q].dma_start(out=out, in_=in_)
        self.cnt[key] += 16
        ins.then_inc(self.sems[key], 16)
        ev = (key, self.cnt[key])
        self._record(ev, reads, writes)
        return ev


def build(n_layers=4, dbg=False):
    nc = bass.Bass("TRN2", target_bir_lowering=False)
    es = ExitStack()
    with es:
        K = Kern(nc, es)
        _build(nc, es, K, n_layers, dbg)
    return nc


def _build(nc, es, K, n_layers, dbg):
    def din(name, shape, dt=F32):
        return nc.dram_tensor(name, list(shape), dt, kind="ExternalInput").ap()

    def dout(name, shape, dt=F32):
        return nc.dram_tensor(name, list(shape), dt, kind="ExternalOutput").ap()

    xp = din("xp", [HALF, D])
    xo = din("xo", [HALF, D])
    memx = din("memx", [256, D])
    w_in_a = din("w_in_a", [2, D, 3328])
    w_out_a = din("w_out_a", [2, D, D])
    w_in_b = din("w_in_b", [2, D, D])
    w_out_b = din("w_out_b", [2, D, D])
    w_mem_kv = din("w_mem_kv", [4, D, 512])
    w_kv = din("w_kv", [D, 512])
    w_up = din("w_up", [4, D, 4096])
    w_down = din("w_down", [4, 4096, D])
    gains_d = din("gains", [128, 14 * 8])
    gfin_d = din("gfin", [128, D])
    sinks_d = din("sinks", [128, 24])
    ident_d = din("ident", [128, 128], BF16)
    maskT_d = din("maskT", [128, 6 * 128])
    wq_d = din("wq", [128, 6 * 128])
    wk_d = din("wk", [128, 6])
    dcur_d = din("dcur", [128, 128], BF16)
    dprev_d = din("dprev", [128, 128], BF16)
    flags_d = din("flags", [128, 2])

    xs_d = din("xs", [128, D])
    state_s = din("state_s", [2, 16, 6, 128, 128])
    memk_s = din("memk_s", [4, 16, 256, 256])
    memv_s = din("memv_s", [4, 16, 256, 256])
    swak_s = din("swak_s", [16, 128, 256])
    swav_s = din("swav_s", [16, 128, 256])
    eyecol_d = din("eyecol", [128, 16])
    eye16_d = din("eye16", [128, 256])
    sel_d = din("sel", [128, 16 * 128], BF16)
    ab_d = din("ab", [128, 12])
    ys_d = dout("ys", [16, D])
    rets_d = dout("rets", [2, 16, 6, 128, 128])
    swako_d = dout("swako", [16, 128, 256])
    swavo_d = dout("swavo", [16, 128, 256])
    y_d = dout("y", [HALF, D])
    ret_d = dout("ret", [2, 6, 128, 128])
    swak_d = dout("swak", [128, 256])
    swav_d = dout("swav", [128, 256])
    memk_d = dout("memk", [4, 256, 256])
    memv_d = dout("memv", [4, 256, 256])

    def sb(name, shape, dt):
        return es.enter_context(nc.sbuf_tensor("sb_" + name, list(shape), dt))

    def ps(name, shape, dt):
        return es.enter_context(nc.psum_tensor("ps_" + name, list(shape), dt))

    xres = sb("xres", [128, NCG, D], F32)
    xres_b = [Buf(f"x{c}") for c in range(NCG)]
    ring = [sb(f"ring{i}", [128, 2048], BF16) for i in range(NSLOT)]
    ring_b = [Buf(f"ring{i}") for i in range(NSLOT)]
    for i in range(NSLOT):
        K.newsem(f"ring{i}")
    ring_pos = [0]

    gains = sb("gains", [128, 14, 8], F32)
    sinks = sb("sinks", [128, 24], F32)
    esink = sb("esink", [128, 24], F32)
    ident = sb("ident", [128, 128], BF16)
    maskT = sb("maskT", [128, 6, 128], F32)
    wq = sb("wq", [128, 6, 128], F32)
    wk = sb("wk", [128, 6], F32)
    dcur = sb("dcur", [128, 128], BF16)
    dprev = sb("dprev", [128, 128], BF16)
    flags = sb("flags", [128, 2], F32)
    epsc = sb("epsc", [128, 1], F32)
    constb = Buf("const", const=True)
    K.newsem("cload")

    S_b = [[Buf(f"S{l}_{h}") for h in range(6)] for l in range(2)]
    Sb_b = [[[Buf(f"Sb{l}_{p}_{h}") for h in range(6)] for p in range(2)] for l in range(2)]
    mk_b = Buf("mkT")
    mv_b = Buf("mv1")

    xs = [sb(f"xs{j}", [128, D], BF16) for j in range(2)]
    xs_b = [Buf(f"xs{j}") for j in range(2)]
    stat = sb("stat", [128, 64], F32)
    stat_b = [Buf(f"stat{j}") for j in range(2)]
    xnTs = [sb(f"xnT{p}", [128, 8, TS], BF16) for p in range(2)]
    xnT_bs = [[Buf(f"xnT{p}_{j}") for j in range(CH)] for p in range(2)]
    xnT = xnTs[0]
    xnT_b = xnT_bs[0]

    def set_xn(p):
        nonlocal xnT, xnT_b
        xnT = xnTs[p]
        xnT_b = xnT_bs[p]
    WK = sb("wkall", [128, 9216], BF16)
    qT = WK[:, 0:1536].rearrange("p (a b) -> p a b", a=6)
    qT_b = Buf("qT")
    qwT = WK[:, 1536:3072].rearrange("p (a b) -> p a b", a=6)
    qwT_b = Buf("qwT")
    kT = WK[:, 3072:4608].rearrange("p (a b) -> p a b", a=6)
    kT_b = Buf("kT")
    qmT = sb("qmT", [128, 2, TS], BF16)
    qmT_b = Buf("qmT")
    tmA = WK[:, 4608:9216].rearrange("p (a b) -> p a b", a=6)
    ktm = [tmA[:, j, :] for j in range(CH)]
    ktm_b = [Buf(f"ktm{j}") for j in range(CH)]
    vtm = [tmA[:, 2 + j, :] for j in range(CH)]
    vtm_b = [Buf(f"vtm{j}") for j in range(CH)]
    gtm = [tmA[:, 4 + j, :] for j in range(CH)]
    gtm_b = [Buf(f"gtm{j}") for j in range(CH)]
    WKB = [qT_b, qwT_b, kT_b] + ktm_b + vtm_b + gtm_b
    scm = sb("scm", [128, 6, 128], BF16)
    scm_b = [Buf("scm_a"), Buf("scm_b")]
    osb = sb("osb", [128, 6, 128], F32)
    osb_b = Buf("osb")
    osq = sb("osq", [128, 6, 128], F32)
    osq_b = Buf("osq")
    gst = sb("gst", [128, 48], F32)
    gst_b = Buf("gst")
    gst2 = sb("gst2", [128, 8], F32)
    gst2_b = Buf("gst2")
    eT = sb("eT", [128, 4, 2, TS], BF16)
    eT_b = [Buf(f"eT{h}") for h in range(4)]
    rden = sb("rden", [128, 16], F32)
    rden_b = Buf("rden")
    cat = [sb(f"cat{j}", [128, D], BF16) for j in range(2)]
    cat_b = [Buf(f"cat{j}") for j in range(2)]
    catT = sb("catT", [128, 8, 128], BF16)
    catT_b = Buf("catT")
    hT = sb("hT", [128, 16, TS], BF16)
    hT_b = [Buf(f"hT{i}") for i in range(16)]
    hrs = [sb(f"hr{i}", [128, TS], BF16) for i in range(2)]
    hrs_b = [Buf(f"hr{i}") for i in range(2)]
    xnall = WK[:, 0:8192].rearrange("p (s k t) -> p s k t", s=NST, k=8)
    xnall_b = [[Buf(f"xnall{s}_{j}") for j in range(CH)] for s in range(NST)]
    yst0 = sb("yst0", [128, D], F32)
    yst = [yst0, yst0]
    yb0 = Buf("yst0")
    yst_b = [yb0, yb0]
    for j in range(2):
        K.newsem(f"yst{j}")
    K.newsem("xload")
    K.newsem("misc_out")
    memx_sb = xres[:, 0:2, :]
    memx_b = xres_b[0]
    mst = yst0[:, 0:512]
    mst_b = yb0
    memnT = xnTs[0]
    memnT_b = xnT_bs[0][0]
    mrstd = sb("mrstd", [128, 8], F32)
    mrstd_b = Buf("mrstd")

    kshT_b = [Buf(f"kshT{c}") for c in range(NCG + 1)]
    vsh_b = [Buf(f"vsh{c}") for c in range(NCG + 1)]
    kvout = yst0[:, 512:1024]
    kvout_b = yb0
    qsT = qT
    qsT_b = qT_b
    st_t = osb[:, 0:4, :]
    st_b = [osb_b, osq_b]
    st_t2 = osq[:, 0:4, :]
    esT = WK[:, 4608:4608 + 3072].rearrange("p (a b c) -> p a b c", a=12, b=2)
    esT_b = [Buf(f"esT{i}") for i in range(6)]

    NPB = 6
    pbank = [ps(f"pb{i}", [128, 512], F32) for i in range(NPB)]
    pbank_b = [Buf(f"pb{i}", excl=True) for i in range(NPB)]
    pbi = [0]
    ptb = [ps(f"ptb{i}", [128, 8, 128], BF16) for i in range(2)]
    ptb_b = [Buf(f"ptb{i}", excl=True) for i in range(2)]
    pti = [0]

    held = set()

    def nextpb():
        while True:
            i = pbi[0]
            pbi[0] = (i + 1) % NPB
            if i not in held:
                return pbank[i], pbank_b[i]

    def holdpb():
        while True:
            i = pbi[0]
            pbi[0] = (i + 1) % NPB
            if i not in held:
                held.add(i)
                return pbank[i], pbank_b[i], i

    def nextpt():
        i = pti[0]
        pti[0] = (i + 1) % 2
        return ptb[i], ptb_b[i]

    def cload(dst, src):
        K.dma("sp", dst, src, "cload", writes=[constb], chain=False)

    cload(gains[:].rearrange("p a b -> p (a b)"), gains_d)
    cload(sinks[:], sinks_d)
    cload(ident[:], ident_d)
    cload(maskT[:].rearrange("p a b -> p (a b)"), maskT_d)
    cload(wq[:].rearrange("p a b -> p (a b)"), wq_d)
    cload(wk[:], wk_d)
    cload(dcur[:], dcur_d)
    cload(dprev[:], dprev_d)
    cload(flags[:], flags_d)
    cev = ("cload", K.cnt["cload"])
    for e in ("pe", "act", "dve"):
        K.eng[e].wait_ge(K.sems["cload"], cev[1])
        K.known[e]["cload"] = cev[1]
    constb.w = None
    initb = Buf("init")
    K.op("dve", lambda v: v.memset(epsc[:], EPS), writes=[initb])
    K.op("act", lambda a: a.activation(out=esink[:], in_=sinks[:], func=AF.Exp), reads=[initb], writes=[initb])
    for e in ("pe", "act", "dve"):
        K._wait(e, [initb], [])

    GI = {"mix": 0, "mlp": 4, "mem": 8, "kv": 12, "fin": 13}
    st_b4 = [[Buf(f"st{hh}_{i}") for i in range(4)] for hh in range(2)]
    ALLWK = WKB + esT_b + [b for row in xnall_b for b in row] + [osb_b, osq_b] + st_b4[0] + st_b4[1]

    def fence():
        for e in ("pe", "act", "dve"):
            K._wait(e, [], ALLWK)

    def load_piece(src_ap, shape):
        i = ring_pos[0]
        ring_pos[0] = (i + 1) % NSLOT
        a, b = shape
        dst = ring[i][:].rearrange("p (a b) -> p a b", a=a)
        K.dma("pool", dst, src_ap, f"ring{i}", writes=[ring_b[i]])
        return i

    def piece_cols(w2d, c0, ncol=256):
        return load_piece(w2d.rearrange("(k p) c -> p k c", p=128)[:, :, c0:c0 + ncol], (8, ncol))

    def piece_rows(w2d, r0, c0):
        return load_piece(w2d[r0:r0 + 512, c0:c0 + 512].rearrange("(k p) c -> p k c", p=128), (4, 512))

    def rv8(i):
        return ring[i][:].rearrange("p (a b) -> p a b", a=8)

    def rv4(i):
        return ring[i][:].rearrange("p (a b) -> p a b", a=4)

    def rms_stats(x_ap, xb, j):
        o = j * 8
        K.op("act", lambda a: a.activation(out=xs[j][:], in_=x_ap, func=AF.Square, scale=1.0 / 32.0,
                                           accum_out=stat[:, o:o + 1]),
             reads=[xb], writes=[xs_b[j], stat_b[j]])
        K.op("act", lambda a: a.activation(out=stat[:, o + 1:o + 2], in_=stat[:, o:o + 1], func=AF.Ln,
                                           bias=epsc[:, 0:1], scale=1.0),
             reads=[stat_b[j]], writes=[stat_b[j]])
        K.op("act", lambda a: a.activation(out=stat[:, o + 2:o + 3], in_=stat[:, o + 1:o + 2], func=AF.Exp,
                                           scale=-0.5),
             reads=[stat_b[j]], writes=[stat_b[j]])
        return stat[:, o + 2:o + 3]

    def norm_A(x_ap, xb, j):
        rstd = rms_stats(x_ap, xb, j)
        K.op("act", lambda a: a.activation(out=xs[j][:], in_=x_ap, func=AF.Copy, scale=rstd),
             reads=[xb, stat_b[j]], writes=[xs_b[j]])

    def norm_B(j, gi, dstT, dst_b):
        pt, ptbuf = nextpt()

        def tr(pe):
            for k in range(8):
                ins = pe.transpose(out=pt[:, k, :], in_=xs[j][:, k * 128:(k + 1) * 128], identity=ident[:])
            return ins
        K.op("pe", tr, reads=[xs_b[j]], writes=[ptbuf])
        K.op("dve", lambda v: v.tensor_tensor(out=dstT, in0=pt[:], in1=gains[:, gi, :].unsqueeze(2).to_broadcast([128, 8, 128]),
                                              op=ALU.mult),
             reads=[ptbuf], writes=[dst_b])

    def norm_T(x_ap, xb, j, gi, dstT, dst_b):
        norm_A(x_ap, xb, j)
        norm_B(j, gi, dstT, dst_b)

    def st_norm_A(st):
        for j in range(CH):
            c = st * CH + j
            norm_A(xres[:, c, :], xres_b[c], j)

    def st_norm_B(gi, p):
        for j in range(CH):
            norm_B(j, gi, xnTs[p][:, :, j * 128:(j + 1) * 128], xnT_bs[p][j])

    def mm_group(pe, out_ap, lhs_fn, rhs_fn, nk):
        for k in range(nk):
            ins = pe.matmul(out_ap, lhsT=lhs_fn(k), rhs=rhs_fn(k), start=(k == 0), stop=(k == nk - 1))
        return ins

    def mem_prologue():
        for mc in range(2):
            rstd = rms_stats(memx_sb[:, mc, :], memx_b, mc)
            K.op("dve", lambda v, mc=mc, rstd=rstd: v.tensor_copy(out=mrstd[:, mc:mc + 1], in_=rstd),
                 reads=[stat_b[mc]], writes=[mrstd_b])
        KSTOP = 99
        if KSTOP < 1:
            return
        for l in range(4):
            if KSTOP < 2 and l > 0:
                return
            p0 = piece_cols(w_mem_kv[l], 0)
            p1 = piece_cols(w_mem_kv[l], 256)
            for mc in range(2):
                K.op("act", lambda a, mc=mc: a.activation(out=xs[mc][:], in_=memx_sb[:, mc, :], func=AF.Copy,
                                                          scale=mrstd[:, mc:mc + 1]),
                     reads=[memx_b, mrstd_b], writes=[xs_b[mc]])
                pt, ptbuf = nextpt()

                def tr(pe, mc=mc, pt=pt):
                    for k in range(8):
                        ins = pe.transpose(out=pt[:, k, :], in_=xs[mc][:, k * 128:(k + 1) * 128], identity=ident[:])
                    return ins
                K.op("pe", tr, reads=[xs_b[mc]], writes=[ptbuf])
                K.op("dve", lambda v, mc=mc, pt=pt, l=l: v.tensor_tensor(
                    out=memnT[:, :, mc * 128:(mc + 1) * 128], in0=pt[:],
                    in1=gains[:, GI["mem"] + l, :].unsqueeze(2).to_broadcast([128, 8, 128]), op=ALU.mult),
                    reads=[ptbuf], writes=[memnT_b])
            if KSTOP < 3:
                continue
            for blk in range(2):
                pb, pbb = nextpb()
                K.op("pe", lambda pe, pb=pb, blk=blk: mm_group(
                    pe, pb[:, 0:256], lambda k: rv8(p0)[:, k, blk * 128:(blk + 1) * 128],
                    lambda k: memnT[:, k, :], 8),
                    reads=[ring_b[p0], memnT_b], writes=[pbb])
                K.op("act", lambda a, pb=pb, blk=blk, l=l: a.activation(out=mkT[:, l, blk, :], in_=pb[:, 0:256], func=AF.Copy),
                     reads=[pbb], writes=[mk_b])
            if KSTOP < 4:
                continue
            for mc in range(2):
                pb, pbb = nextpb()

                def tm(pe, pb=pb, mc=mc):
                    mm_group(pe, pb[:, 0:256], lambda k: memnT[:, k, mc * 128:(mc + 1) * 128], lambda k: rv8(p0)[:, k, :], 8)
                    return mm_group(pe, pb[:, 256:512], lambda k: memnT[:, k, mc * 128:(mc + 1) * 128], lambda k: rv8(p1)[:, k, :], 8)
                K.op("pe", tm, reads=[ring_b[p0], ring_b[p1], memnT_b], writes=[pbb])
                K.op("act", lambda a, pb=pb: a.activation(out=mst[:], in_=pb[:], func=AF.Copy), reads=[pbb], writes=[mst_b])
                if KSTOP != 5:
                    K.op("dve", lambda v, pb=pb, mc=mc, l=l: v.tensor_copy(
                        out=mv1[:, l, mc, :, 0:64], in_=pb[:, 256:512].rearrange("p (h d) -> p h d", h=4)),
                        reads=[pbb], writes=[mv_b])
                if KSTOP != 6:
                    K.dma("sp", memk_d[l, mc * 128:(mc + 1) * 128, :], mst[:, 0:256], "misc_out", reads=[mst_b])
                    K.dma("sp", memv_d[l, mc * 128:(mc + 1) * 128, :], mst[:, 256:512], "misc_out", reads=[mst_b])

    def state_update(l, j):
        pKa, pKab = nextpb()
        pKb, pKbb = nextpb()

        def kvm(pe, pk, hs):
            for i, h in enumerate(hs):
                ins = pe.matmul(pk[:, i * 128:(i + 1) * 128], lhsT=ktm[j][:, h * 128:(h + 1) * 128], rhs=vtm[j][:, h * 128:(h + 1) * 128], start=True, stop=True)
            return ins
        K.op("pe", lambda pe: kvm(pe, pKa, (0, 1, 2, 3)), reads=[ktm_b[j], vtm_b[j]], writes=[pKab])
        K.op("pe", lambda pe: kvm(pe, pKb, (4, 5)), reads=[ktm_b[j], vtm_b[j]], writes=[pKbb])
        for h in range(6):
            pk, pkb, i = (pKa, pKab, h) if h < 4 else (pKb, pKbb, h - 4)
            K.op("dve", lambda v, pk=pk, i=i, h=h: v.scalar_tensor_tensor(
                out=S[l][:, h, :], in0=S[l][:, h, :], scalar=GC[h], in1=pk[:, i * 128:(i + 1) * 128],
                op0=ALU.mult, op1=ALU.add),
                reads=[pkb, S_b[l][h]], writes=[S_b[l][h]])
            K.op("act", lambda a, h=h: a.activation(out=Sb[l][(j + 1) % 2][:, h, :], in_=S[l][:, h, :], func=AF.Copy),
                 reads=[S_b[l][h]], writes=[Sb_b[l][(j + 1) % 2][h]])

    def mixer_a(l, full_sts=None):
        fence()
        sts = list(range(NST))
        if full_sts is None:
            full_sts = set(sts)
        wi = w_in_a[l]
        if full_sts:
            pin = [piece_cols(wi, c * 256) for c in range(13)]
            wo = w_out_a[l]
            pout = [[piece_rows(wo, rg * 512, ch * 512) for rg in range(2)] for ch in range(2)]
        else:
            pin = {c: piece_cols(wi, c * 256) for c in range(3, 9)}
            pout = None
        gi = GI["mix"] + l
        pcur = 0
        st_norm_A(sts[0])
        st_norm_B(gi, pcur)
        set_xn(pcur)
        a_inproj_main(l, sts[0], sts[0] in full_sts, pin)
        if sts[0] in full_sts:
            a_inproj_gate(l, sts[0], pin)
        for idx, st in enumerate(sts):
            full = st in full_sts
            nxt = sts[idx + 1] if idx + 1 < len(sts) else None
            if nxt is not None:
                st_norm_A(nxt)
            pnext = 1 - pcur
            a_body(l, st, full, mid=(lambda: st_norm_B(gi, pnext)) if nxt is not None else None)
            if nxt is not None:
                pcur = pnext
                set_xn(pcur)
                a_inproj_main(l, nxt, nxt in full_sts, pin)
            if full:
                for j in range(CH):
                    out_proj(st * CH + j, cat[j], cat_b[j], pout)
            if nxt is not None and nxt in full_sts:
                a_inproj_gate(l, nxt, pin)
        set_xn(0)

    def a_inproj_main(l, st, full, pin):
        if True:
            for blk in (range(14) if full else ()):
                col = blk * 128 if blk < 12 else 3072 + (blk - 12) * 128
                pi = pin[col // 256]
                co = col % 256
                pb, pbb = nextpb()
                K.op("pe", lambda pe, pb=pb, pi=pi, co=co: mm_group(
                    pe, pb[:, 0:TS], lambda k: rv8(pi)[:, k, co:co + 128], lambda k: xnT[:, k, :], 8),
                    reads=[ring_b[pi]] + xnT_b, writes=[pbb])
                if blk < 6:
                    h = blk
                    K.op("act", lambda a, pb=pb, h=h: a.activation(out=qT[:, h, :], in_=pb[:, 0:TS], func=AF.Copy),
                         reads=[pbb], writes=[qT_b])
                    K.op("dve", lambda v, pb=pb, h=h: v.tensor_tensor(
                        out=qwT[:, h, :].rearrange("p (c q) -> p c q", c=CH),
                        in0=pb[:, 0:TS].rearrange("p (c q) -> p c q", c=CH),
                        in1=wq[:, h, :].unsqueeze(1).to_broadcast([128, CH, 128]), op=ALU.mult),
                        reads=[pbb], writes=[qwT_b])
                elif blk < 12:
                    h = blk - 6
                    K.op("act", lambda a, pb=pb, h=h: a.activation(out=kT[:, h, :], in_=pb[:, 0:TS], func=AF.Copy),
                         reads=[pbb], writes=[kT_b])
                else:
                    b2 = blk - 12
                    K.op("act", lambda a, pb=pb, b2=b2: a.activation(out=qmT[:, b2, :], in_=pb[:, 0:TS], func=AF.Copy),
                         reads=[pbb], writes=[qmT_b])
            for j in range(CH):
                for pp in range(3, 9):
                    pi = pin[pp]
                    pb, pbb = nextpb()
                    K.op("pe", lambda pe, pb=pb, pi=pi, j=j: mm_group(
                        pe, pb[:, 0:256], lambda k: xnT[:, k, j * 128:(j + 1) * 128], lambda k: rv8(pi)[:, k, :], 8),
                        reads=[ring_b[pi], xnT_b[j]], writes=[pbb])
                    cc = (pp - 3) % 3 * 256
                    if pp < 6:
                        h0 = cc // 128
                        K.op("dve", lambda v, pb=pb, j=j, cc=cc, h0=h0: v.tensor_tensor(
                            out=ktm[j][:, cc:cc + 256].rearrange("p (h d) -> p h d", h=2),
                            in0=pb[:, 0:256].rearrange("p (h d) -> p h d", h=2),
                            in1=wk[:, h0:h0 + 2].unsqueeze(2).to_broadcast([128, 2, 128]), op=ALU.mult),
                            reads=[pbb], writes=[ktm_b[j]])
                    elif pp < 9:
                        K.op("act", lambda a, pb=pb, j=j, cc=cc: a.activation(out=vtm[j][:, cc:cc + 256], in_=pb[:, 0:256], func=AF.Copy),
                             reads=[pbb], writes=[vtm_b[j]])
                    else:
                        K.op("act", lambda a, pb=pb, j=j, cc=cc: a.activation(out=gtm[j][:, cc:cc + 256], in_=pb[:, 0:256], func=AF.Silu),
                             reads=[pbb], writes=[gtm_b[j]])

    def a_inproj_gate(l, st, pin):
        if True:
            for j in range(CH):
                for pp in range(9, 12):
                    pi = pin[pp]
                    pb, pbb = nextpb()
                    K.op("pe", lambda pe, pb=pb, pi=pi, j=j: mm_group(
                        pe, pb[:, 0:256], lambda k: xnT[:, k, j * 128:(j + 1) * 128], lambda k: rv8(pi)[:, k, :], 8),
                        reads=[ring_b[pi], xnT_b[j]], writes=[pbb])
                    cc = (pp - 3) % 3 * 256
                    if pp < 6:
                        h0 = cc // 128
                        K.op("dve", lambda v, pb=pb, j=j, cc=cc, h0=h0: v.tensor_tensor(
                            out=ktm[j][:, cc:cc + 256].rearrange("p (h d) -> p h d", h=2),
                            in0=pb[:, 0:256].rearrange("p (h d) -> p h d", h=2),
                            in1=wk[:, h0:h0 + 2].unsqueeze(2).to_broadcast([128, 2, 128]), op=ALU.mult),
                            reads=[pbb], writes=[ktm_b[j]])
                    elif pp < 9:
                        K.op("act", lambda a, pb=pb, j=j, cc=cc: a.activation(out=vtm[j][:, cc:cc + 256], in_=pb[:, 0:256], func=AF.Copy),
                             reads=[pbb], writes=[vtm_b[j]])
                    else:
                        K.op("act", lambda a, pb=pb, j=j, cc=cc: a.activation(out=gtm[j][:, cc:cc + 256], in_=pb[:, 0:256], func=AF.Silu),
                             reads=[pbb], writes=[gtm_b[j]])

    def a_body(l, st, full, mid=None):
        if True:
            for hm in (range(4) if full else ()):
                pb, pbb = nextpb()
                lo = (hm % 2) * 64

                def sc(pe, pb=pb, hm=hm, lo=lo):
                    for mc in range(2):
                        ins = pe.matmul(pb[:, mc * TS:(mc + 1) * TS], lhsT=mkT[lo:lo + 64, l, hm // 2, mc * 128:(mc + 1) * 128],
                                        rhs=qmT[lo:lo + 64, hm // 2, :], start=True, stop=True)
                    return ins
                K.op("pe", sc, reads=[mk_b, qmT_b], writes=[pbb])
                K.op("act", lambda a, pb=pb, hm=hm: a.activation(out=eT[:, hm, :, :].rearrange("p a b -> p (a b)"), in_=pb[:], func=AF.Exp, scale=0.125),
                     reads=[pbb], writes=[eT_b[hm]])
            pOs = {}
            for j in range(CH):
                js = slice(j * 128, (j + 1) * 128)
                state_update(l, j)
                if not full:
                    continue
                pA, pAb = nextpb()
                pB, pBb = nextpb()

                def scr(pe, pA=pA, hs=(0, 1, 2, 3), js=js):
                    for i, h in enumerate(hs):
                        ins = pe.matmul(pA[:, i * 128:(i + 1) * 128], lhsT=kT[:, h, js], rhs=qT[:, h, js], start=True, stop=True)
                    return ins
                K.op("pe", scr, reads=[kT_b, qT_b], writes=[pAb])
                K.op("pe", lambda pe, pB=pB, js=js: scr(pe, pB, (4, 5), js), reads=[kT_b, qT_b], writes=[pBb])
                K.op("dve", lambda v, pA=pA: v.tensor_tensor(out=scm[:, 0:4, :], in0=pA[:].rearrange("p (h q) -> p h q", h=4),
                                                             in1=maskT[:, 0:4, :], op=ALU.mult),
                     reads=[pAb], writes=[scm_b[0]])
                K.op("dve", lambda v, pB=pB: v.tensor_tensor(out=scm[:, 4:6, :], in0=pB[:, 0:256].rearrange("p (h q) -> p h q", h=2),
                                                             in1=maskT[:, 4:6, :], op=ALU.mult),
                     reads=[pBb], writes=[scm_b[1]])
                pOa, pOab, ia = holdpb()
                pOb, pObb, ib = holdpb()

                def omm(pe, po, hs, j=j, js=js):
                    for i, h in enumerate(hs):
                        pe.matmul(po[:, i * 128:(i + 1) * 128], lhsT=scm[:, h, :], rhs=vtm[j][:, h * 128:(h + 1) * 128], start=True, stop=False)
                        ins = pe.matmul(po[:, i * 128:(i + 1) * 128], lhsT=qwT[:, h, js], rhs=Sb[l][j % 2][:, h, :], start=False, stop=True)
                    return ins
                K.op("pe", lambda pe, pOa=pOa: omm(pe, pOa, (0, 1, 2, 3)), reads=[scm_b[0], vtm_b[j], qwT_b] + Sb_b[l][j % 2][0:4], writes=[pOab])
                K.op("pe", lambda pe, pOb=pOb: omm(pe, pOb, (4, 5)), reads=[scm_b[1], vtm_b[j], qwT_b] + Sb_b[l][j % 2][4:6], writes=[pObb])
                pOs[j] = (pOa, pOab, pOb, pObb, ia, ib)
            if mid is not None:
                mid()
            for j in (range(CH) if full else ()):
                js = slice(j * 128, (j + 1) * 128)
                cb = cat[j]
                cbb = cat_b[j]
                pOa, pOab, pOb, pObb, ia, ib = pOs[j]
                gn_gate(pOa, pOab, pOb, pObb, cb, cbb, gtm[j], gtm_b[j], 128)
                held.discard(ia)
                held.discard(ib)
                pM, pMb = nextpb()

                def mo(pe, pM=pM, js=js):
                    for hm in range(4):
                        for mc in range(2):
                            ins = pe.matmul(pM[:, hm * 65:(hm + 1) * 65], lhsT=eT[:, hm, mc, js], rhs=mv1[:, l, mc, hm, :],
                                            start=(mc == 0), stop=(mc == 1))
                    return ins
                K.op("pe", mo, reads=eT_b + [mv_b], writes=[pMb])
                mem_out(pM, pMb, cb, cbb)

    def gn_full(pOa, pOab, pOb, pObb, cb, cbb, gate, gate_b):
        K.op("dve", lambda v: v.tensor_reduce(out=gst[:, 0:4], in_=pOa[:].rearrange("p (h e) -> p h e", h=4), axis=AX.X, op=ALU.add),
             reads=[pOab], writes=[gst_b])
        K.op("dve", lambda v: v.tensor_reduce(out=gst[:, 4:6], in_=pOb[:, 0:256].rearrange("p (h e) -> p h e", h=2), axis=AX.X, op=ALU.add),
             reads=[pObb], writes=[gst_b])
        for h in range(6):
            po, pob, i = (pOa, pOab, h) if h < 4 else (pOb, pObb, h - 4)
            K.op("act", lambda a, po=po, i=i, h=h: a.activation(out=hrs[0][:, 0:128], in_=po[:, i * 128:(i + 1) * 128], func=AF.Square,
                                                              accum_out=gst2[:, h:h + 1]),
                 reads=[pob], writes=[hrs_b[0], gst2_b])
        K.op("dve", lambda v: v.tensor_scalar(out=gst[:, 12:18], in0=gst[:, 0:6], scalar1=1.0 / 128.0, scalar2=None, op0=ALU.mult),
             reads=[gst_b], writes=[gst_b])
        K.op("dve", lambda v: v.tensor_tensor(out=gst[:, 18:24], in0=gst[:, 12:18], in1=gst[:, 12:18], op=ALU.mult),
             reads=[gst_b], writes=[gst_b])
        K.op("dve", lambda v: v.scalar_tensor_tensor(out=gst[:, 24:30], in0=gst2[:, 0:6], scalar=1.0 / 128.0, in1=gst[:, 18:24],
                                                      op0=ALU.mult, op1=ALU.subtract),
             reads=[gst_b, gst2_b], writes=[gst_b])
        K.op("act", lambda a: a.activation(out=gst[:, 30:36], in_=gst[:, 24:30], func=AF.Ln, bias=epsc[:, 0:1], scale=1.0),
             reads=[gst_b], writes=[gst_b])
        K.op("act", lambda a: a.activation(out=gst[:, 36:42], in_=gst[:, 30:36], func=AF.Exp, scale=-0.5),
             reads=[gst_b], writes=[gst_b])
        K.op("dve", lambda v: v.scalar_tensor_tensor(out=gst[:, 42:48], in0=gst[:, 12:18], scalar=-1.0, in1=gst[:, 36:42],
                                                      op0=ALU.mult, op1=ALU.mult),
             reads=[gst_b], writes=[gst_b])
        for h in range(6):
            po, pob, i = (pOa, pOab, h) if h < 4 else (pOb, pObb, h - 4)
            K.op("act", lambda a, po=po, i=i, h=h: a.activation(out=cb[:, h * 128:(h + 1) * 128], in_=po[:, i * 128:(i + 1) * 128], func=AF.Identity,
                                                              scale=gst[:, 36 + h:37 + h], bias=gst[:, 42 + h:43 + h]),
                 reads=[pob, gst_b], writes=[cbb])
        K.op("dve", lambda v: v.tensor_tensor(out=cb[:, 0:MIX], in0=cb[:, 0:MIX], in1=gate[:, :], op=ALU.mult),
             reads=[cbb, gate_b], writes=[cbb])

    def gn_gate(pOa, pOab, pOb, pObb, cb, cbb, gate, gate_b, P):
        K.op("act", lambda a: a.activation(out=osb[0:P, 0:4, :].rearrange("p a b -> p (a b)"), in_=pOa[0:P, :], func=AF.Copy),
             reads=[pOab], writes=[osb_b])
        K.op("act", lambda a: a.activation(out=osb[0:P, 4:6, :].rearrange("p a b -> p (a b)"), in_=pOb[0:P, 0:256], func=AF.Copy),
             reads=[pObb, osb_b], writes=[osb_b])
        K.op("dve", lambda v: v.tensor_reduce(out=gst[0:P, 0:6], in_=osb[0:P], axis=AX.X, op=ALU.add),
             reads=[osb_b], writes=[gst_b])
        K.op("dve", lambda v: v.tensor_tensor(out=osq[0:P], in0=osb[0:P], in1=osb[0:P], op=ALU.mult),
             reads=[osb_b], writes=[osq_b])
        K.op("dve", lambda v: v.tensor_reduce(out=gst[0:P, 6:12], in_=osq[0:P], axis=AX.X, op=ALU.add),
             reads=[osq_b, gst_b], writes=[gst_b])
        K.op("dve", lambda v: v.tensor_scalar(out=gst[0:P, 12:18], in0=gst[0:P, 0:6], scalar1=1.0 / 128.0, scalar2=None, op0=ALU.mult),
             reads=[gst_b], writes=[gst_b])
        K.op("dve", lambda v: v.tensor_tensor(out=gst[0:P, 18:24], in0=gst[0:P, 12:18], in1=gst[0:P, 12:18], op=ALU.mult),
             reads=[gst_b], writes=[gst_b])
        K.op("dve", lambda v: v.scalar_tensor_tensor(out=gst[0:P, 24:30], in0=gst[0:P, 6:12], scalar=1.0 / 128.0, in1=gst[0:P, 18:24],
                                                      op0=ALU.mult, op1=ALU.subtract),
             reads=[gst_b], writes=[gst_b])
        K.op("act", lambda a: a.activation(out=gst[0:P, 30:36], in_=gst[0:P, 24:30], func=AF.Ln, bias=epsc[0:P, 0:1], scale=1.0),
             reads=[gst_b], writes=[gst_b])
        K.op("act", lambda a: a.activation(out=gst[0:P, 36:42], in_=gst[0:P, 30:36], func=AF.Exp, scale=-0.5),
             reads=[gst_b], writes=[gst_b])
        K.op("dve", lambda v: v.tensor_tensor(out=osb[0:P], in0=osb[0:P], in1=gst[0:P, 12:18].unsqueeze(2).to_broadcast([P, 6, 128]), op=ALU.subtract),
             reads=[osb_b, gst_b], writes=[osb_b])
        K.op("dve", lambda v: v.tensor_tensor(out=osb[0:P], in0=osb[0:P], in1=gst[0:P, 36:42].unsqueeze(2).to_broadcast([P, 6, 128]), op=ALU.mult),
             reads=[osb_b, gst_b], writes=[osb_b])
        K.op("dve", lambda v: v.tensor_tensor(out=cb[0:P, 0:MIX], in0=osb[0:P].rearrange("p a b -> p (a b)"), in1=gate[0:P, :], op=ALU.mult),
             reads=[osb_b, gate_b], writes=[cbb])

    def mem_out(pM, pMb, cb, cbb, P=128):
        pv = pM[0:P, 0:260].rearrange("p (h d) -> p h d", h=4)
        K.op("dve", lambda v: v.reciprocal(out=rden[0:P, 0:4], in_=pv[:, :, 64]), reads=[pMb], writes=[rden_b])
        K.op("dve", lambda v: v.tensor_tensor(out=cb[0:P, MIX:D].rearrange("p (h d) -> p h d", h=4), in0=pv[:, :, 0:64],
                                              in1=rden[0:P, 0:4].unsqueeze(2).to_broadcast([P, 4, 64]), op=ALU.mult),
             reads=[pMb, rden_b], writes=[cbb])

    def out_proj(c, cb, cbb, pout):
        pt, ptbuf = nextpt()

        def tr(pe):
            for k in range(8):
                ins = pe.transpose(out=pt[:, k, :], in_=cb[:, k * 128:(k + 1) * 128], identity=ident[:])
            return ins
        K.op("pe", tr, reads=[cbb], writes=[ptbuf])
        K.op("act", lambda a: a.activation(out=catT[:].rearrange("p a b -> p (a b)"), in_=pt[:].rearrange("p a b -> p (a b)"), func=AF.Copy),
             reads=[ptbuf], writes=[catT_b])
        for ch in range(2):
            pb, pbb = nextpb()
            K.op("pe", lambda pe, pb=pb, ch=ch: mm_group(
                pe, pb[:], lambda k: catT[:, k, :], lambda k: rv4(pout[ch][k // 4])[:, k % 4, :], 8),
                reads=[catT_b, ring_b[pout[ch][0]], ring_b[pout[ch][1]]], writes=[pbb])
            K.op("dve", lambda v, pb=pb, ch=ch: v.tensor_tensor(out=xres[:, c, ch * 512:(ch + 1) * 512], in0=xres[:, c, ch * 512:(ch + 1) * 512],
                                                                 in1=pb[:], op=ALU.add),
                 reads=[pbb, xres_b[c]], writes=[xres_b[c]])

    def mlp(l, nst=NST, sts=None):
        fence()
        gi = GI["mlp"] + l
        if sts is None:
            sts = list(range(nst))

        def mnormA(st):
            for j in range(CH):
                c = st * CH + j
                norm_A(xres[:, c, :], xres_b[c], j)

        def mnormB(st):
            for j in range(CH):
                norm_B(j, gi, xnall[:, st, :, j * 128:(j + 1) * 128], xnall_b[st][j])
        for hf in range(2):
            pup = [piece_cols(w_up[l], hf * 2048 + i * 256) for i in range(8)]
            pdn = [[piece_rows(w_down[l], hf * 2048 + rg * 512, ch * 512) for rg in range(4)] for ch in range(2)]
            if hf == 0:
                mnormA(sts[0])
                mnormB(sts[0])
            for idx, st in enumerate(sts):
                if hf == 0 and idx + 1 < len(sts):
                    mnormA(sts[idx + 1])
                for f in range(16):
                    pi = pup[f // 2]
                    co = (f % 2) * 128
                    pb, pbb = nextpb()
                    K.op("pe", lambda pe, pb=pb, pi=pi, co=co: mm_group(
                        pe, pb[:, 0:TS], lambda k: rv8(pi)[:, k, co:co + 128], lambda k: xnall[:, st, k, :], 8),
                        reads=[ring_b[pi]] + xnall_b[st], writes=[pbb])
                    hrt, hrb = hrs[f % 2], hrs_b[f % 2]
                    K.op("act", lambda a, pb=pb, hrt=hrt: a.activation(out=hrt[:], in_=pb[:, 0:TS], func=AF.Relu), reads=[pbb], writes=[hrb])
                    K.op("dve", lambda v, f=f, hrt=hrt: v.tensor_tensor(out=hT[:, f, :], in0=hrt[:], in1=hrt[:], op=ALU.mult),
                         reads=[hrb], writes=[hT_b[f]])
                if hf == 0 and idx + 1 < len(sts):
                    mnormB(sts[idx + 1])
                for j in range(CH):
                    c = st * CH + j
                    for ch in range(2):
                        pb, pbb = nextpb()
                        K.op("pe", lambda pe, pb=pb, ch=ch, j=j: mm_group(
                            pe, pb[:], lambda k: hT[:, k, j * 128:(j + 1) * 128], lambda k: rv4(pdn[ch][k // 4])[:, k % 4, :], 16),
                            reads=hT_b + [ring_b[i] for i in pdn[ch]], writes=[pbb])
                        K.op("dve", lambda v, pb=pb, ch=ch, c=c: v.tensor_tensor(
                            out=xres[:, c, ch * 512:(ch + 1) * 512], in0=xres[:, c, ch * 512:(ch + 1) * 512], in1=pb[:], op=ALU.add),
                            reads=[pbb, xres_b[c]], writes=[xres_b[c]])

    KDUP = [(0, 0), (0, 1), (1, 1), (2, 2), (2, 3), (3, 3)]

    def kv_phase(last_group, sts=None):
        p0 = piece_cols(w_kv, 0)
        p1 = piece_cols(w_kv, 256)
        gi = GI["kv"]
        if sts is None:
            sts = list(range(NST))
        pcur = 0
        st_norm_A(sts[0])
        st_norm_B(gi, pcur)
        for idx, st in enumerate(sts):
            set_xn(pcur)
            if idx + 1 < len(sts):
                st_norm_A(sts[idx + 1])
                st_norm_B(gi, 1 - pcur)
            pcur = 1 - pcur
            for bi, (ga, gb) in enumerate(KDUP):
                pb, pbb = nextpb()

                def kf(pe, pb=pb, ga=ga, gb=gb):
                    for half, g in ((0, ga), (1, gb)):
                        for k in range(8):
                            ins = pe.matmul(pb[half * 64:(half + 1) * 64, 0:TS], lhsT=rv8(p0)[:, k, g * 64:(g + 1) * 64], rhs=xnT[:, k, :],
                                            start=(k == 0), stop=(k == 7))
                    return ins
                K.op("pe", kf, reads=[ring_b[p0]] + xnT_b, writes=[pbb])
                K.op("act", lambda a, pb=pb, bi=bi, st=st: a.activation(out=kshT[:, bi, 128 + st * TS:128 + (st + 1) * TS], in_=pb[:, 0:TS], func=AF.Copy),
                     reads=[pbb], writes=[kshT_b[1 + st * CH + jj] for jj in range(CH)])
            for j in range(CH):
                c = st * CH + j
                pb, pbb = nextpb()

                def tm(pe, pb=pb, j=j):
                    mm_group(pe, pb[:, 0:256], lambda k: xnT[:, k, j * 128:(j + 1) * 128], lambda k: rv8(p0)[:, k, :], 8)
                    return mm_group(pe, pb[:, 256:512], lambda k: xnT[:, k, j * 128:(j + 1) * 128], lambda k: rv8(p1)[:, k, :], 8)
                K.op("pe", tm, reads=[ring_b[p0], ring_b[p1], xnT_b[j]], writes=[pbb])
                K.op("dve", lambda v, pb=pb, c=c: v.tensor_copy(out=vsh1[:, 1 + c, :, 0:64], in_=pb[:, 256:512].rearrange("p (h d) -> p h d", h=4)),
                     reads=[pbb], writes=[vsh_b[1 + c]])
                if last_group and c == NCG - 1:
                    K.op("act", lambda a, pb=pb: a.activation(out=kvout[:], in_=pb[:], func=AF.Copy), reads=[pbb], writes=[kvout_b])
                    K.dma("sp", swak_d, kvout[:, 0:256], "misc_out", reads=[kvout_b])
                    K.dma("sp", swav_d, kvout[:, 256:512], "misc_out", reads=[kvout_b])
        set_xn(0)

    def kv_shift():
        K.op("dve", lambda v: v.tensor_copy(out=kshT[:, :, 0:128], in_=kshT[:, :, NCG * 128:(NCG + 1) * 128]),
             reads=[kshT_b[NCG]], writes=[kshT_b[0]])
        K.op("dve", lambda v: v.tensor_copy(out=vsh1[:, 0, :, 0:64], in_=vsh1[:, NCG, :, 0:64]),
             reads=[vsh_b[NCG]], writes=[vsh_b[0]])

    def mixer_b(l, first_group):
        fence()
        jb = l - 2
        wi = w_in_b[jb]
        pin = [piece_cols(wi, c * 256) for c in range(4)]
        wo = w_out_b[jb]
        pout = [[piece_rows(wo, rg * 512, ch * 512) for rg in range(2)] for ch in range(2)]
        gi = GI["mix"] + l
        pcur = 0
        st_norm_A(0)
        st_norm_B(gi, pcur)
        for st in range(NST):
            set_xn(pcur)
            pnext = 1 - pcur
            pcur = pnext
            for blk in range(8):
                col = blk * 128
                pi = pin[col // 256]
                co = col % 256
                pb, pbb = nextpb()
                K.op("pe", lambda pe, pb=pb, pi=pi, co=co: mm_group(
                    pe, pb[:, 0:TS], lambda k: rv8(pi)[:, k, co:co + 128], lambda k: xnT[:, k, :], 8),
                    reads=[ring_b[pi]] + xnT_b, writes=[pbb])
                if blk < 6:
                    K.op("act", lambda a, pb=pb, blk=blk: a.activation(out=qsT[:, blk, :], in_=pb[:, 0:TS], func=AF.Copy),
                         reads=[pbb], writes=[qsT_b])
                else:
                    b2 = blk - 6
                    K.op("act", lambda a, pb=pb, b2=b2: a.activation(out=qmT[:, b2, :], in_=pb[:, 0:TS], func=AF.Copy),
                         reads=[pbb], writes=[qmT_b])
            for hm in range(4):
                pb, pbb = nextpb()
                lo = (hm % 2) * 64

                def sc(pe, pb=pb, hm=hm, lo=lo):
                    for mc in range(2):
                        ins = pe.matmul(pb[:, mc * TS:(mc + 1) * TS], lhsT=mkT[lo:lo + 64, l, hm // 2, mc * 128:(mc + 1) * 128],
                                        rhs=qmT[lo:lo + 64, hm // 2, :], start=True, stop=True)
                    return ins
                K.op("pe", sc, reads=[mk_b, qmT_b], writes=[pbb])
                K.op("act", lambda a, pb=pb, hm=hm: a.activation(out=eT[:, hm, :, :].rearrange("p a b -> p (a b)"), in_=pb[:], func=AF.Exp, scale=0.125),
                     reads=[pbb], writes=[eT_b[hm]])
            if st + 1 < NST:
                st_norm_A(st + 1)
            MB = 99
            for j in range(CH):
                if MB < 1:
                    continue
                c = st * CH + j
                js = slice(j * 128, (j + 1) * 128)
                cb = cat[j]
                cbb = cat_b[j]
                masked_prev = first_group and c == 0
                for hpp in range(3):
                    for hh in range(2):
                        pb, pbb = nextpb()
                        lo = hh * 64

                        def sw(pe, pb=pb, hpp=hpp, lo=lo, js=js, c=c):
                            for i in range(2):
                                hp = 2 * hpp + i
                                for kb in range(2):
                                    kc0 = (c + kb) * 128
                                    ins = pe.matmul(pb[:, (i * 2 + kb) * 128:(i * 2 + kb + 1) * 128],
                                                    lhsT=kshT[lo:lo + 64, hp, kc0:kc0 + 128], rhs=qsT[lo:lo + 64, hp, js], start=True, stop=True)
                            return ins
                        K.op("pe", sw, reads=[kshT_b[c], kshT_b[c + 1], qsT_b], writes=[pbb])
                        stb = st_t if hh == 0 else st_t2
                        stbb = st_b[hh]
                        for i in range(2):
                            hq = (2 * hpp + i) * 2 + hh
                            for kb in range(2):
                                idx = i * 2 + kb
                                dm = dprev if kb == 0 else dcur
                                K.op("dve", lambda v, pb=pb, idx=idx, dm=dm, hq=hq, stb=stb: v.scalar_tensor_tensor(
                                    out=stb[:, idx, :], in0=dm[:], scalar=8.0 * SLOPES[hq], in1=pb[:, idx * 128:(idx + 1) * 128],
                                    op0=ALU.mult, op1=ALU.add),
                                    reads=[pbb], writes=[st_b4[hh][idx]])
                        for i in range(2):
                            hp = 2 * hpp + i
                            hq = hp * 2 + hh
                            if masked_prev:
                                K.op("act", lambda a, i=i, hq=hq, stb=stb: a.activation(out=esT[:, hq, 0, :], in_=stb[:, i * 2, :], func=AF.Exp,
                                                                                       scale=0.125, bias=flags[:, 1:2]),
                                     reads=[st_b4[hh][i * 2]], writes=[esT_b[hp]])
                                K.op("act", lambda a, i=i, hq=hq, stb=stb: a.activation(out=esT[:, hq, 1, :], in_=stb[:, i * 2 + 1, :], func=AF.Exp,
                                                                                       scale=0.125),
                                     reads=[st_b4[hh][i * 2 + 1]], writes=[esT_b[hp]])
                            else:
                                K.op("act", lambda a, i=i, hq=hq, stb=stb: a.activation(out=esT[:, hq, :, :], in_=stb[:, i * 2:i * 2 + 2, :], func=AF.Exp, scale=0.125),
                                     reads=[st_b4[hh][i * 2], st_b4[hh][i * 2 + 1]], writes=[esT_b[hp]])
                if MB < 2:
                    continue
                pOs = [nextpb(), nextpb()]
                for half in range(2):
                    po, pob = pOs[half]

                    def so(pe, po=po, half=half, c=c):
                        for i in range(6):
                            hq = half * 6 + i
                            g = hq // 3
                            for kb in range(2):
                                ins = pe.matmul(po[:, i * 65:(i + 1) * 65], lhsT=esT[:, hq, kb, :], rhs=vsh1[:, c + kb, g, :],
                                                start=(kb == 0), stop=(kb == 1))
                        return ins
                    K.op("pe", so, reads=esT_b[half * 3:half * 3 + 3] + [vsh_b[c], vsh_b[c + 1]], writes=[pob])
                    pv = po[:, 0:390].rearrange("p (h d) -> p h d", h=6)
                    K.op("dve", lambda v, pv=pv, half=half: v.tensor_tensor(out=rden[:, 4:10], in0=pv[:, :, 64],
                                                                           in1=esink[:, jb * 12 + half * 6:jb * 12 + half * 6 + 6], op=ALU.add),
                         reads=[pob], writes=[rden_b])
                    K.op("dve", lambda v: v.reciprocal(out=rden[:, 10:16], in_=rden[:, 4:10]), reads=[rden_b], writes=[rden_b])
                    K.op("dve", lambda v, pv=pv, half=half, cb=cb: v.tensor_tensor(
                        out=cb[:, half * 384:(half + 1) * 384].rearrange("p (h d) -> p h d", h=6), in0=pv[:, :, 0:64],
                        in1=rden[:, 10:16].unsqueeze(2).to_broadcast([128, 6, 64]), op=ALU.mult),
                        reads=[pob, rden_b], writes=[cbb])
                if MB < 3:
                    continue
                pM, pMb = nextpb()

                def mo(pe, pM=pM, js=js):
                    for hm in range(4):
                        for mc in range(2):
                            ins = pe.matmul(pM[:, hm * 65:(hm + 1) * 65], lhsT=eT[:, hm, mc, js], rhs=mv1[:, l, mc, hm, :],
                                            start=(mc == 0), stop=(mc == 1))
                    return ins
                K.op("pe", mo, reads=eT_b + [mv_b], writes=[pMb])
                mem_out(pM, pMb, cb, cbb)
            if st + 1 < NST:
                st_norm_B(gi, pnext)
            for j in range(CH):
                out_proj(st * CH + j, cat[j], cat_b[j], pout)
        set_xn(0)

    def final_out(grp_own):
        for c in range(NCG):
            j = c % 2
            if dbg:
                K.dma("sp", y_d[grp_own * G + c * 128:grp_own * G + (c + 1) * 128, :], xres[:, c, :], f"yst{j}", reads=[xres_b[c]])
                continue
            rstd = rms_stats(xres[:, c, :], xres_b[c], j)
            K.dma("sp", yst[j][:], gfin_d, f"yst{j}", writes=[yst_b[j]])
            K.op("dve", lambda v, c=c, j=j, rstd=rstd: v.scalar_tensor_tensor(out=yst[j][:], in0=xres[:, c, :], scalar=rstd, in1=yst[j][:],
                                                                             op0=ALU.mult, op1=ALU.mult),
                 reads=[xres_b[c], stat_b[j], yst_b[j]], writes=[yst_b[j]])
            K.dma("sp", y_d[grp_own * G + c * 128:grp_own * G + (c + 1) * 128, :], yst[j][:], f"yst{j}", reads=[yst_b[j]])


    GAM = [float(np.exp(LOGG[h])) for h in range(6)]

    def sample_phase():
        es_s = ExitStack()

        def sbs(name, shape, dt):
            return es_s.enter_context(nc.sbuf_tensor("ss_" + name, list(shape), dt))
        eyecol = sbs("eyecol", [128, 16], F32)
        eye16 = sbs("eye16", [128, 16, 16], F32)
        SEL = sbs("SEL", [128, 16, 128], BF16)
        AB = sbs("AB", [128, 12], F32)
        S0 = [sbs(f"S0_{i}", [128, 6, 128], F32) for i in range(2)]
        S0_b = [Buf(f"S0_{i}") for i in range(2)]
        Snb = [sbs(f"Snb{i}", [128, 6, 128], BF16) for i in range(2)]
        Snb_b = [Buf(f"Snb{i}") for i in range(2)]
        km = [sbs(f"km{i}", [128, MIX], BF16) for i in range(2)]
        km_b = [Buf(f"km{i}") for i in range(2)]
        QS = sbs("QS", [128, 16, 6, 16], BF16)
        QS_b = Buf("QS")
        Kc = [sbs(f"Kc{i}", [128, 2, 256], F32) for i in range(2)]
        Kc_b = [Buf(f"Kc{i}") for i in range(2)]
        Vc = [sbs(f"Vc{i}", [128, 2, 256], F32) for i in range(2)]
        Vc_b = [Buf(f"Vc{i}") for i in range(2)]
        V1 = [sbs(f"V1{i}", [128, 2, 4, 65], BF16) for i in range(2)]
        V1_b = [Buf(f"V1{i}") for i in range(2)]
        scs = sbs("scs", [128, 48], F32)
        scs_b = Buf("scs")
        es_t = sbs("es_t", [128, 16], F32)
        es_b = Buf("es_t")
        Esel = [sbs(f"Esel{i}", [128, 12, 16], BF16) for i in range(2)]
        Esel_b = [Buf(f"Esel{i}") for i in range(2)]
        qm_tm = sbs("qm_tm", [128, 256], BF16)
        qm_b = Buf("qm_tm")
        q_tmb = sbs("q_tmb", [128, MIX], BF16)
        q_b = Buf("q_tmb")
        kvn = sbs("kvn", [128, 512], F32)
        kvn_b = Buf("kvn")
        en = sbs("en", [128, 16], F32)
        en_b = Buf("en")
        skeys = []
        for i in range(2):
            for nm in ("S0", "Kc", "Vc"):
                K.newsem(f"{nm}{i}")
                skeys.append(f"{nm}{i}")
        prodf = osq[:].rearrange("p a b -> p (a b)")
        osbf = osb[:].rearrange("p a b -> p (a b)")
        sinit = Buf("sinit")
        K.newsem("cload2")
        K.dma("sp", eyecol[:], eyecol_d, "cload2", writes=[sinit], chain=False)
        K.dma("sp", eye16[:].rearrange("p a b -> p (a b)"), eye16_d, "cload2", writes=[sinit], chain=False)
        K.dma("sp", SEL[:].rearrange("p a b -> p (a b)"), sel_d, "cload2", writes=[sinit], chain=False)
        K.dma("sp", AB[:], ab_d, "cload2", writes=[sinit], chain=False)
        for e in ("pe", "act", "dve"):
            K._wait(e, [sinit], [])
        K.dma("sp", xres[:, 0, :], xs_d, "xload", writes=[xres_b[0]])
        K.op("dve", lambda v: v.memset(xres[:, 1, :], 0.0), writes=[xres_b[1]])
        K.op("dve", lambda v: v.memset(cat[0][:], 0.0), writes=[cat_b[0]])
        for i in range(2):
            K.op("dve", lambda v, i=i: v.memset(V1[i][:].rearrange("p a b c -> p (a b c)"), 1.0), writes=[V1_b[i]])

        def tm_piece(pi, width=256):
            pb, pbb = nextpb()
            K.op("pe", lambda pe: mm_group(pe, pb[:, 0:width], lambda k: xnT[:, k, 0:128], lambda k: rv8(pi)[:, k, 0:width], 8),
                 reads=[ring_b[pi], xnT_b[0]], writes=[pbb])
            return pb, pbb

        def s_mem_attn(l):
            pM, pMb, im = holdpb()
            K.op("dve", lambda v: v.memset(pM[0:16, :], 0.0), writes=[pMb])
            pbs = {}

            def stA(s):
                bf = s % 2
                K.dma("sp", Kc[bf][:], memk_s[l, s].rearrange("(c p) f -> p c f", p=128), f"Kc{bf}", writes=[Kc_b[bf]])
                K.dma("sp", Vc[bf][:], memv_s[l, s].rearrange("(c p) f -> p c f", p=128), f"Vc{bf}", writes=[Vc_b[bf]])
                K.op("act", lambda a: a.activation(out=V1[bf][:, :, :, 0:64], in_=Vc[bf][:].rearrange("p c (h d) -> p c h d", h=4), func=AF.Copy),
                     reads=[Vc_b[bf]], writes=[V1_b[bf]])
                pb, pbb = nextpb()
                K.op("pe", lambda pe: pe.matmul(pb[:, 0:256], lhsT=SEL[:, s, :], rhs=qm_tm[:], start=True, stop=True),
                     reads=[qm_b], writes=[pbb])
                pbs[s] = (pb, pbb)

            def stB(s):
                bf = s % 2
                pb, pbb = pbs.pop(s)
                K.op("dve", lambda v: v.tensor_tensor(out=prodf[:, 0:512].rearrange("p (c f) -> p c f", c=2), in0=Kc[bf][:],
                                                      in1=pb[:, 0:256].unsqueeze(1).to_broadcast([128, 2, 256]), op=ALU.mult),
                     reads=[pbb, Kc_b[bf]], writes=[osq_b])
                K.op("dve", lambda v: v.tensor_reduce(out=scs[:, 0:8], in_=prodf[:, 0:512].rearrange("p (g d) -> p g d", d=64), axis=AX.X, op=ALU.add),
                     reads=[osq_b], writes=[scs_b])
                K.op("act", lambda a: a.activation(out=es_t[:, 0:8], in_=scs[:, 0:8], func=AF.Exp, scale=0.125), reads=[scs_b], writes=[es_b])
                K.op("dve", lambda v: v.tensor_tensor(out=Esel[bf][:, 0:8, :], in0=es_t[:, 0:8].unsqueeze(2).to_broadcast([128, 8, 16]),
                                                      in1=eye16[:, s, :].unsqueeze(1).to_broadcast([128, 8, 16]), op=ALU.mult),
                     reads=[es_b], writes=[Esel_b[bf]])

            def stC(s):
                bf = s % 2

                def pv(pe):
                    for h in range(4):
                        for mc in range(2):
                            ins = pe.matmul(pM[0:16, h * 65:(h + 1) * 65], lhsT=Esel[bf][:, mc * 4 + h, :], rhs=V1[bf][:, mc, h, :],
                                            start=False, stop=(s == 15 and mc == 1), skip_group_check=True)
                    return ins
                K.op("pe", pv, reads=[Esel_b[bf], V1_b[bf]], writes=[pMb] if s in (0, 15) else [])
            stA(0)
            for s in range(16):
                if s + 1 < 16:
                    stA(s + 1)
                stB(s)
                stC(s)
            mem_out(pM, pMb, cat[0], cat_b[0], P=16)
            held.discard(im)

        def s_mixer_a(l):
            fence()
            pin = [piece_cols(w_in_a[l], c * 256) for c in range(13)]
            pout = [[piece_rows(w_out_a[l], rg * 512, ch * 512) for rg in range(2)] for ch in range(2)]
            norm_T(xres[:, 0, :], xres_b[0], 0, GI["mix"] + l, xnT[:, :, 0:128], xnT_b[0])
            for h in range(6):
                col = h * 128
                pi = pin[col // 256]
                co = col % 256
                pb, pbb = nextpb()
                K.op("pe", lambda pe, pb=pb, pi=pi, co=co: mm_group(pe, pb[:, 0:128], lambda k: rv8(pi)[:, k, co:co + 128], lambda k: xnT[:, k, 0:128], 8),
                     reads=[ring_b[pi], xnT_b[0]], writes=[pbb])
                K.op("dve", lambda v, pb=pb, h=h: v.tensor_tensor(out=QS[:, :, h, :], in0=pb[:, 0:16].unsqueeze(1).to_broadcast([128, 16, 16]),
                                                                   in1=eye16[:], op=ALU.mult),
                     reads=[pbb], writes=[QS_b])
            for pp in range(3, 13):
                pb, pbb = tm_piece(pin[pp])
                cc = (pp - 3) % 3 * 256
                if pp < 6:
                    K.op("act", lambda a, pb=pb, cc=cc: a.activation(out=ktm[0][:, cc:cc + 256], in_=pb[:, 0:256], func=AF.Copy), reads=[pbb], writes=[ktm_b[0]])
                elif pp < 9:
                    K.op("act", lambda a, pb=pb, cc=cc: a.activation(out=vtm[0][:, cc:cc + 256], in_=pb[:, 0:256], func=AF.Copy), reads=[pbb], writes=[vtm_b[0]])
                elif pp < 12:
                    K.op("act", lambda a, pb=pb, cc=cc: a.activation(out=gtm[0][:, cc:cc + 256], in_=pb[:, 0:256], func=AF.Silu), reads=[pbb], writes=[gtm_b[0]])
                else:
                    K.op("act", lambda a, pb=pb: a.activation(out=qm_tm[:], in_=pb[:, 0:256], func=AF.Copy), reads=[pbb], writes=[qm_b])
            pOa, pOab, ia = holdpb()
            pOb, pObb, ib = holdpb()
            K.op("dve", lambda v: v.memset(pOa[0:16, :], 0.0), writes=[pOab])
            K.op("dve", lambda v: v.memset(pOb[0:16, :], 0.0), writes=[pObb])
            pks = {}

            def rA(s):
                bf = s % 2
                K.dma("sp", S0[bf][:], state_s[l, s].rearrange("h d e -> d h e"), f"S0{bf}", writes=[S0_b[bf]])
                K.op("dve", lambda v: v.tensor_scalar(out=km[bf][:], in0=ktm[0][:, :], scalar1=eyecol[:, s:s + 1], scalar2=None, op0=ALU.mult),
                     reads=[ktm_b[0]], writes=[km_b[bf]])
                pKa, pKab = nextpb()
                pKb, pKbb = nextpb()

                def kvm(pe, pk, hs):
                    for i, h in enumerate(hs):
                        ins = pe.matmul(pk[:, i * 128:(i + 1) * 128], lhsT=km[bf][:, h * 128:(h + 1) * 128], rhs=vtm[0][:, h * 128:(h + 1) * 128], start=True, stop=True)
                    return ins
                K.op("pe", lambda pe: kvm(pe, pKa, (0, 1, 2, 3)), reads=[km_b[bf], vtm_b[0]], writes=[pKab])
                K.op("pe", lambda pe: kvm(pe, pKb, (4, 5)), reads=[km_b[bf], vtm_b[0]], writes=[pKbb])
                pks[s] = (pKa, pKab, pKb, pKbb)

            def rB(s):
                bf = s % 2
                pKa, pKab, pKb, pKbb = pks.pop(s)
                for h in range(6):
                    pk, pkb, i = (pKa, pKab, h) if h < 4 else (pKb, pKbb, h - 4)
                    K.op("dve", lambda v, pk=pk, i=i, h=h: v.scalar_tensor_tensor(
                        out=S0[bf][:, h, :], in0=S0[bf][:, h, :], scalar=GAM[h], in1=pk[:, i * 128:(i + 1) * 128], op0=ALU.mult, op1=ALU.add),
                        reads=[pkb, S0_b[bf]], writes=[S0_b[bf]])
                K.dma("sp", rets_d[l, s].rearrange("h d e -> d h e"), S0[bf][:], f"S0{bf}", reads=[S0_b[bf]])
                K.op("act", lambda a: a.activation(out=Snb[bf][:].rearrange("p a b -> p (a b)"), in_=S0[bf][:].rearrange("p a b -> p (a b)"), func=AF.Copy),
                     reads=[S0_b[bf]], writes=[Snb_b[bf]])

            def rC(s):
                bf = s % 2

                def om(pe):
                    for h in range(6):
                        po, i = (pOa, h) if h < 4 else (pOb, h - 4)
                        ins = pe.matmul(po[0:16, i * 128:(i + 1) * 128], lhsT=QS[:, s, h, :], rhs=Snb[bf][:, h, :], start=False, stop=(s == 15), skip_group_check=True)
                    return ins
                K.op("pe", om, reads=[QS_b, Snb_b[bf]], writes=[pOab, pObb] if s in (0, 15) else [])
            rA(0)
            for s in range(16):
                if s + 1 < 16:
                    rA(s + 1)
                rB(s)
                rC(s)
            gn_gate(pOa, pOab, pOb, pObb, cat[0], cat_b[0], gtm[0], gtm_b[0], 16)
            held.discard(ia)
            held.discard(ib)
            s_mem_attn(l)
            out_proj(0, cat[0], cat_b[0], pout)

        def s_kv_phase():
            p0 = piece_cols(w_kv, 0)
            p1 = piece_cols(w_kv, 256)
            norm_T(xres[:, 0, :], xres_b[0], 0, GI["kv"], xnT[:, :, 0:128], xnT_b[0])
            pb, pbb = nextpb()

            def tm(pe):
                mm_group(pe, pb[:, 0:256], lambda k: xnT[:, k, 0:128], lambda k: rv8(p0)[:, k, :], 8)
                return mm_group(pe, pb[:, 256:512], lambda k: xnT[:, k, 0:128], lambda k: rv8(p1)[:, k, :], 8)
            K.op("pe", tm, reads=[ring_b[p0], ring_b[p1], xnT_b[0]], writes=[pbb])
            K.op("act", lambda a: a.activation(out=kvn[:], in_=pb[:], func=AF.Copy), reads=[pbb], writes=[kvn_b])
            K.dma("sp", swako_d[:, 127, :], kvn[0:16, 0:256], "misc_out", reads=[kvn_b])
            K.dma("sp", swavo_d[:, 127, :], kvn[0:16, 256:512], "misc_out", reads=[kvn_b])
            K.dma("sp", swako_d[:, 0:127, :], swak_s[:, 1:128, :], "misc_out")
            K.dma("sp", swavo_d[:, 0:127, :], swav_s[:, 1:128, :], "misc_out")

        def s_mixer_b(l):
            fence()
            jb = l - 2
            pin = [piece_cols(w_in_b[jb], c * 256) for c in range(4)]
            pout = [[piece_rows(w_out_b[jb], rg * 512, ch * 512) for rg in range(2)] for ch in range(2)]
            norm_T(xres[:, 0, :], xres_b[0], 0, GI["mix"] + l, xnT[:, :, 0:128], xnT_b[0])
            for pp in range(4):
                pb, pbb = tm_piece(pin[pp])
                if pp < 3:
                    K.op("act", lambda a, pb=pb, pp=pp: a.activation(out=q_tmb[:, pp * 256:(pp + 1) * 256], in_=pb[:, 0:256], func=AF.Copy), reads=[pbb], writes=[q_b])
                else:
                    K.op("act", lambda a, pb=pb: a.activation(out=qm_tm[:], in_=pb[:, 0:256], func=AF.Copy), reads=[pbb], writes=[qm_b])
            K.op("dve", lambda v: v.tensor_tensor(out=prodf[0:16, :].rearrange("p (g r d) -> p g r d", g=4, r=3),
                                                  in0=q_tmb[0:16, :].rearrange("p (g r d) -> p g r d", g=4, r=3),
                                                  in1=kvn[0:16, 0:256].rearrange("p (g d) -> p g d", g=4).unsqueeze(2).to_broadcast([16, 4, 3, 64]), op=ALU.mult),
                 reads=[q_b, kvn_b], writes=[osq_b])
            K.op("dve", lambda v: v.tensor_reduce(out=scs[0:16, 16:28], in_=prodf[0:16, :].rearrange("p (g d) -> p g d", d=64), axis=AX.X, op=ALU.add),
                 reads=[osq_b], writes=[scs_b])
            K.op("act", lambda a: a.activation(out=en[0:16, 0:12], in_=scs[0:16, 16:28], func=AF.Exp, scale=0.125), reads=[scs_b], writes=[en_b])
            pSa, pSab, ia = holdpb()
            pSb, pSbb, ib = holdpb()
            K.op("dve", lambda v: v.memset(pSa[0:16, :], 0.0), writes=[pSab])
            K.op("dve", lambda v: v.memset(pSb[0:16, :], 0.0), writes=[pSbb])
            qbs = {}

            def wA(s):
                bf = s % 2
                K.dma("sp", Kc[bf][:, 0, :], swak_s[s], f"Kc{bf}", writes=[Kc_b[bf]])
                K.dma("sp", Vc[bf][:, 0, :], swav_s[s], f"Vc{bf}", writes=[Vc_b[bf]])
                K.op("act", lambda a: a.activation(out=V1[bf][:, 0, :, 0:64], in_=Vc[bf][:, 0, :].rearrange("p (h d) -> p h d", h=4), func=AF.Copy),
                     reads=[Vc_b[bf]], writes=[V1_b[bf]])
                lst = []
                for half in range(2):
                    pb, pbb = nextpb()
                    K.op("pe", lambda pe, pb=pb, half=half: pe.matmul(pb[:, 0:384], lhsT=SEL[:, s, :], rhs=q_tmb[:, half * 384:(half + 1) * 384], start=True, stop=True),
                         reads=[q_b], writes=[pbb])
                    lst.append((pb, pbb))
                qbs[s] = lst

            def wB(s):
                bf = s % 2
                lst = qbs.pop(s)
                for half in range(2):
                    pb, pbb = lst[half]
                    K.op("dve", lambda v, pb=pb, half=half: v.tensor_tensor(
                        out=prodf[:, half * 384:(half + 1) * 384].rearrange("p (g r d) -> p g r d", g=2, r=3),
                        in0=Kc[bf][:, 0, half * 128:(half + 1) * 128].rearrange("p (g d) -> p g d", g=2).unsqueeze(2).to_broadcast([128, 2, 3, 64]),
                        in1=pb[:, 0:384].rearrange("p (g r d) -> p g r d", g=2, r=3), op=ALU.mult),
                        reads=[pbb, Kc_b[bf]], writes=[osq_b])
                K.op("dve", lambda v: v.tensor_reduce(out=scs[:, 0:12], in_=prodf[:, :].rearrange("p (g d) -> p g d", d=64), axis=AX.X, op=ALU.add),
                     reads=[osq_b], writes=[scs_b])
                K.op("dve", lambda v: v.scalar_tensor_tensor(out=scs[:, 32:44], in0=scs[:, 0:12], scalar=0.125, in1=AB[:], op0=ALU.mult, op1=ALU.add),
                     reads=[scs_b], writes=[scs_b])
                K.op("act", lambda a: a.activation(out=es_t[:, 0:12], in_=scs[:, 32:44], func=AF.Exp), reads=[scs_b], writes=[es_b])
                K.op("dve", lambda v: v.tensor_tensor(out=Esel[bf][:, 0:12, :], in0=es_t[:, 0:12].unsqueeze(2).to_broadcast([128, 12, 16]),
                                                      in1=eye16[:, s, :].unsqueeze(1).to_broadcast([128, 12, 16]), op=ALU.mult),
                     reads=[es_b], writes=[Esel_b[bf]])

            def wC(s):
                bf = s % 2

                def pv(pe):
                    for hq in range(12):
                        po = pSa if hq < 6 else pSb
                        i = hq % 6
                        ins = pe.matmul(po[0:16, i * 65:(i + 1) * 65], lhsT=Esel[bf][:, hq, :], rhs=V1[bf][:, 0, hq // 3, :], start=False, stop=(s == 15), skip_group_check=True)
                    return ins
                K.op("pe", pv, reads=[Esel_b[bf], V1_b[bf]], writes=[pSab, pSbb] if s in (0, 15) else [])
            wA(0)
            for s in range(16):
                if s + 1 < 16:
                    wA(s + 1)
                wB(s)
                wC(s)
            for half in range(2):
                po, pob = (pSa, pSab) if half == 0 else (pSb, pSbb)
                pvw = po[0:16, 0:390].rearrange("p (h d) -> p h d", h=6)
                t1 = osbf[0:16, 0:384].rearrange("p (g r d) -> p g r d", g=2, r=3)
                K.op("dve", lambda v, half=half, t1=t1: v.tensor_tensor(
                    out=t1, in0=kvn[0:16, 256 + half * 128:256 + (half + 1) * 128].rearrange("p (g d) -> p g d", g=2).unsqueeze(2).to_broadcast([16, 2, 3, 64]),
                    in1=en[0:16, half * 6:(half + 1) * 6].rearrange("p (g r) -> p g r", g=2).unsqueeze(3).to_broadcast([16, 2, 3, 64]), op=ALU.mult),
                    reads=[kvn_b, en_b], writes=[osb_b])
                t1f = osbf[0:16, 0:384].rearrange("p (h d) -> p h d", h=6)
                K.op("dve", lambda v, t1f=t1f, pvw=pvw: v.tensor_tensor(out=t1f, in0=t1f, in1=pvw[:, :, 0:64], op=ALU.add),
                     reads=[pob, osb_b], writes=[osb_b])
                K.op("dve", lambda v, pvw=pvw, half=half: v.tensor_tensor(out=rden[0:16, 4:10], in0=pvw[:, :, 64], in1=en[0:16, half * 6:(half + 1) * 6], op=ALU.add),
                     reads=[pob, en_b], writes=[rden_b])
                K.op("dve", lambda v, half=half: v.tensor_tensor(out=rden[0:16, 4:10], in0=rden[0:16, 4:10],
                                                                 in1=esink[0:16, jb * 12 + half * 6:jb * 12 + half * 6 + 6], op=ALU.add),
                     reads=[rden_b], writes=[rden_b])
                K.op("dve", lambda v: v.reciprocal(out=rden[0:16, 10:16], in_=rden[0:16, 4:10]), reads=[rden_b], writes=[rden_b])
                K.op("dve", lambda v, half=half, t1f=t1f: v.tensor_tensor(
                    out=cat[0][0:16, half * 384:(half + 1) * 384].rearrange("p (h d) -> p h d", h=6), in0=t1f,
                    in1=rden[0:16, 10:16].unsqueeze(2).to_broadcast([16, 6, 64]), op=ALU.mult),
                    reads=[osb_b, rden_b], writes=[cat_b[0]])
            held.discard(ia)
            held.discard(ib)
            s_mem_attn(l)
            out_proj(0, cat[0], cat_b[0], pout)

        for l in range(min(n_layers, 4)):
            if l < 2:
                s_mixer_a(l)
            else:
                s_mixer_b(l)
            mlp(l, nst=1)
            if l == 1:
                s_kv_phase()
        if dbg:
            K.dma("sp", ys_d, xres[0:16, 0, :], "yst0", reads=[xres_b[0]])
        else:
            rstd = rms_stats(xres[:, 0, :], xres_b[0], 0)
            K.dma("sp", yst0[:], gfin_d, "yst0", writes=[yb0])
            K.op("dve", lambda v: v.scalar_tensor_tensor(out=yst0[:], in0=xres[:, 0, :], scalar=rstd, in1=yst0[:], op0=ALU.mult, op1=ALU.mult),
                 reads=[xres_b[0], stat_b[0], yb0], writes=[yb0])
            K.dma("sp", ys_d, yst0[0:16, :], "yst0", reads=[yb0])
        keys = skeys + ["yst0", "misc_out", "xload", "cload", "cload2"]
        for e in ("pe", "act", "dve", "sp"):
            for f in ("pe", "act", "dve"):
                if f != e:
                    kf = K.cur[f]
                    if K.cnt[kf] > 0 and K.known[e].get(kf, 0) < K.cnt[kf]:
                        K.eng[e].wait_ge(K.sems[kf], K.cnt[kf])
                        K.known[e][kf] = K.cnt[kf]
            for kk in keys:
                if K.cnt[kk] > 0 and K.known[e].get(kk, 0) < K.cnt[kk]:
                    K.eng[e].wait_ge(K.sems[kk], K.cnt[kk])
                    K.known[e][kk] = K.cnt[kk]
        es_s.close()

    STAGE = 99
    if True:
        sample_phase()
    S = [sb(f"S{l}", [128, 6, 128], F32) for l in range(2)]
    Sb = [[sb(f"Sb{l}_{p}", [128, 6, 128], BF16) for p in range(2)] for l in range(2)]
    mkT = sb("mkT", [128, 4, 2, 256], BF16)
    mv1 = sb("mv1", [128, 4, 2, 4, 65], BF16)
    kshT = sb("kshT", [128, 6, (NCG + 1) * 128], BF16)
    vsh1 = sb("vsh1", [128, NCG + 1, 4, 65], BF16)
    K.op("dve", lambda v: v.memset(mv1[:].rearrange("p a b c d -> p (a b c d)"), 1.0), writes=[mv_b])
    K.op("dve", lambda v: v.memset(vsh1[:].rearrange("p a b c -> p (a b c)"), 1.0), writes=vsh_b)
    K.op("dve", lambda v: v.memset(kshT[:].rearrange("p a b -> p (a b)"), 0.0), writes=kshT_b)
    for l in range(2):
        K.op("dve", lambda v, l=l: v.memset(S[l][:].rearrange("p a b -> p (a b)"), 0.0), writes=S_b[l])
        for p_ in range(2):
            K.op("dve", lambda v, l=l, p_=p_: v.memset(Sb[l][p_][:].rearrange("p a b -> p (a b)"), 0.0), writes=Sb_b[l][p_])
    K.dma("sp", memx_sb[:], memx.rearrange("(c p) d -> p c d", p=128), "xload", writes=[xres_b[0], xres_b[1]])
    if STAGE >= 0:
        mem_prologue()
    for grp in range(4):
        if STAGE < 1:
            break
        isP = grp < 2
        src = xp if isP else xo
        g0 = (grp % 2) * G
        K.dma("sp", xres[:], src[g0:g0 + G, :].rearrange("(c p) d -> p c d", p=128), "xload", writes=xres_b)
        if grp == 2:
            for l in range(2):
                K.op("dve", lambda v, l=l: v.tensor_scalar(out=S[l][:].rearrange("p a b -> p (a b)"), in0=S[l][:].rearrange("p a b -> p (a b)"),
                                                          scalar1=flags[:, 0:1], scalar2=None, op0=ALU.mult),
                     reads=S_b[l], writes=S_b[l])
                K.op("act", lambda a, l=l: a.activation(out=Sb[l][0][:].rearrange("p a b -> p (a b)"), in_=S[l][:].rearrange("p a b -> p (a b)"), func=AF.Copy),
                     reads=S_b[l], writes=Sb_b[l][0])
        layers = [0, 1] if isP else [0, 1, 2, 3]
        layers = [l for l in layers if l < n_layers]
        for l in layers:
            if STAGE < 2:
                break
            if isP and l == 1:
                if grp == 0:
                    mixer_a(1, full_sts=set())
                else:
                    mixer_a(1, full_sts={NST - 1})
                    mlp(1, sts=[NST - 1])
                    kv_phase(last_group=False, sts=[NST - 1])
                continue
            if l < 2:
                mixer_a(l)
            else:
                mixer_b(l, first_group=(grp == 2))
            if STAGE < 3:
                continue
            mlp(l)
            if l == 1:
                kv_phase(last_group=(grp == 3))
        if 1 in layers and STAGE >= 3 and grp >= 1:
            kv_shift()
        if not isP:
            final_out(grp - 2)
    for l in range(2):
        K.dma("sp", ret_d[l].rearrange("h d e -> d h e"), S[l][:], "misc_out", reads=S_b[l])
    fin = {}
    for key in ["misc_out", "yst0", "yst1"]:
        if K.cnt[key] > 0:
            nc.sync.wait_ge(K.sems[key], K.cnt[key])
    for e in ("pe", "act", "dve"):
        key = K.cur[e]
        if K.cnt[key] > 0:
            nc.sync.wait_ge(K.sems[key], K.cnt[key])
    for i in range(NSLOT):
        if K.cnt[f"ring{i}"] > 0:
            nc.sync.wait_ge(K.sems[f"ring{i}"], K.cnt[f"ring{i}"])
    print("ops:", K.nops, len(K.sems))


def host_consts():
    c = {}
    c["ident"] = np.eye(128, dtype=np.float32).astype(ml_dtypes.bfloat16)
    q = np.arange(128)
    maskT = np.zeros((128, 6, 128), np.float32)
    wq = np.zeros((128, 6, 128), np.float32)
    wk = np.zeros((128, 6), np.float32)
    for h in range(6):
        diff = q[None, :] - q[:, None]
        m = np.where(diff >= 0, np.exp(np.maximum(diff, 0) * LOGG[h]), 0.0) * (128.0 ** -0.5)
        maskT[:, h, :] = m
        wq[:, h, :] = np.exp((q + 1.0) * LOGG[h])[None, :]
        wk[:, h] = np.exp((127.0 - q) * LOGG[h]) * (128.0 ** -0.5)
    c["maskT"] = maskT.reshape(128, 768)
    c["wq"] = wq.reshape(128, 768)
    c["wk"] = wk
    kk = q[:, None].astype(np.float32)
    qq = q[None, :].astype(np.float32)
    NEG = -1.0e7
    c["dcur"] = np.where(qq >= kk, kk - qq, NEG).astype(np.float32).astype(ml_dtypes.bfloat16)
    c["dprev"] = np.where(kk >= qq, kk - qq - 128.0, NEG).astype(np.float32).astype(ml_dtypes.bfloat16)
    eyecol = np.zeros((128, 16), np.float32)
    eye16 = np.zeros((128, 16, 16), np.float32)
    sel = np.zeros((128, 16, 128), np.float32)
    for s_ in range(16):
        eyecol[s_, s_] = 128.0 ** -0.5
        eye16[:, s_, s_] = 1.0
        sel[s_, s_, :] = 1.0
    c["eyecol"] = eyecol
    c["eye16"] = eye16.reshape(128, 256)
    c["sel"] = sel.reshape(128, 16 * 128).astype(ml_dtypes.bfloat16)
    ab = np.zeros((128, 12), np.float32)
    for hq in range(12):
        ab[:, hq] = -SLOPES[hq] * (128.0 - q)
    c["ab"] = ab
    return c


def kernel(x_prompt, x_sample, cache_mem_k, cache_mem_v, state_ret, cache_swa_k, cache_swa_v, mem_prompt,
           norm_mix, w_in_a, w_out_a, w_in_b, w_out_b, attn_sinks, norm_mem, w_mem_kv, norm_kv, w_kv,
           norm_mlp, w_up, w_down, norm_final, _n_layers=4, _dbg=False, _ncores=8):
    f = lambda a: np.ascontiguousarray(np.asarray(a, dtype=np.float32))
    x_prompt = f(x_prompt)
    consts = host_consts()
    gvecs = np.concatenate([f(norm_mix), f(norm_mlp), f(norm_mem), f(norm_kv)[None], f(norm_final)[None]], axis=0)
    gains = np.ascontiguousarray(gvecs.reshape(14, 8, 128).transpose(2, 0, 1)).reshape(128, 14 * 8)
    gfin = np.ascontiguousarray(np.broadcast_to(f(norm_final)[None, :], (128, D)))
    sinks = np.ascontiguousarray(np.broadcast_to(f(attn_sinks).reshape(1, 24), (128, 24)))
    shared = dict(
        w_in_a=f(w_in_a), w_out_a=f(w_out_a), w_in_b=f(w_in_b), w_out_b=f(w_out_b), w_mem_kv=f(w_mem_kv), w_kv=f(w_kv),
        w_up=f(w_up), w_down=f(w_down), gains=gains, gfin=gfin, sinks=sinks, **consts)
    in_maps = []
    for c in range(8):
        b, half = c // 2, c % 2
        flags = np.zeros((128, 2), np.float32)
        flags[:, 0] = 1.0 if half == 1 else 0.0
        flags[:, 1] = 0.0 if half == 1 else -1.0e9
        m = dict(shared)
        m["xp"] = np.ascontiguousarray(x_prompt[b, 0:HALF])
        m["xo"] = np.ascontiguousarray(x_prompt[b, half * HALF:(half + 1) * HALF])
        m["memx"] = f(mem_prompt)[b]
        m["flags"] = flags
        xs = np.zeros((128, D), np.float32)
        xs[0:16] = f(x_sample)[16 * c:16 * c + 16, 0, :]
        m["xs"] = xs
        m["state_s"] = np.ascontiguousarray(f(state_ret)[:, 16 * c:16 * c + 16])
        m["memk_s"] = np.ascontiguousarray(f(cache_mem_k)[:, 16 * c:16 * c + 16]).reshape(4, 16, 256, 256)
        m["memv_s"] = np.ascontiguousarray(f(cache_mem_v)[:, 16 * c:16 * c + 16]).reshape(4, 16, 256, 256)
        m["swak_s"] = np.ascontiguousarray(f(cache_swa_k)[16 * c:16 * c + 16]).reshape(16, 128, 256)
        m["swav_s"] = np.ascontiguousarray(f(cache_swa_v)[16 * c:16 * c + 16]).reshape(16, 128, 256)
        in_maps.append(m)
    nc = build(_n_layers, _dbg)
    if _ncores < 8:
        res = run_bass_kernel_spmd(nc, in_maps[:_ncores], core_ids=list(range(_ncores)))
        R = list(res.results)
        while len(R) < 8:
            R.append(R[len(R) % _ncores])
    else:
        res = run_bass_kernel_spmd(nc, in_maps, core_ids=list(range(8)))
        R = res.results
    y_prompt = np.stack([np.concatenate([R[2 * b]["y"], R[2 * b + 1]["y"]], axis=0) for b in range(4)])
    ret_prompt = np.stack([R[2 * b + 1]["ret"] for b in range(4)], axis=1)
    swa_k = np.stack([R[2 * b + 1]["swak"].reshape(128, 4, 64) for b in range(4)])
    swa_v = np.stack([R[2 * b + 1]["swav"].reshape(128, 4, 64) for b in range(4)])
    mem_k = np.stack([R[2 * b]["memk"].reshape(4, 256, 4, 64) for b in range(4)], axis=1)
    mem_v = np.stack([R[2 * b]["memv"].reshape(4, 256, 4, 64) for b in range(4)], axis=1)
    y_sample = np.concatenate([R[c]["ys"] for c in range(8)], axis=0).reshape(128, 1, D)
    ret_sample = np.concatenate([R[c]["rets"] for c in range(8)], axis=1)
    swa_ks = np.concatenate([R[c]["swako"] for c in range(8)], axis=0).reshape(128, 128, 4, 64)
    swa_vs = np.concatenate([R[c]["swavo"] for c in range(8)], axis=0).reshape(128, 128, 4, 64)
    return (y_prompt, y_sample, ret_prompt, ret_sample, swa_k, swa_v, swa_ks, swa_vs, mem_k, mem_v)
```

```python
import math
import numpy as np
from contextlib import ExitStack
import ml_dtypes
import concourse.bass as bass
import concourse.mybir as mybir
from concourse.bass_utils import run_bass_kernel_spmd

F32 = mybir.dt.float32
BF16 = mybir.dt.bfloat16
ALU = mybir.AluOpType
AF = mybir.ActivationFunctionType
AX = mybir.AxisListType

D = 1024
SEQ = 4096
HALF = 2048
G = 1024
NCG = G // 128
TS = 256
CH = TS // 128
NST = G // TS
DEPTH = 4
MIX = 768
NSLOT = 17
EPS = 1e-6
RET_H = 6
GAMMA = [1.0 - 2.0 ** (-5.0 - h) for h in range(RET_H)]
LOGG = [float(np.log1p(-(2.0 ** (-5.0 - h)))) for h in range(RET_H)]
GC = [float(np.exp(128 * LOGG[h])) for h in range(RET_H)]


def alibi_slopes(n):
    def pow2(m):
        start = 2.0 ** (-8.0 / m)
        return [start ** (i + 1) for i in range(m)]
    if math.log2(n).is_integer():
        s = pow2(n)
    else:
        c = 2 ** int(math.floor(math.log2(n)))
        s = pow2(c) + pow2(2 * c)[0::2][: n - c]
    return [float(np.float32(v)) for v in s]


SLOPES = alibi_slopes(12)


class Buf:
    __slots__ = ("name", "w", "r", "const", "excl")

    def __init__(self, name, const=False, excl=False):
        self.name = name
        self.w = None
        self.r = {}
        self.const = const
        self.excl = excl


class Kern:
    def __init__(self, nc, es):
        self.nc = nc
        self.es = es
        self.eng = {"pe": nc.tensor, "act": nc.scalar, "dve": nc.vector, "pool": nc.gpsimd, "sp": nc.sync}
        self.sems = {}
        self.cnt = {}
        self.known = {e: {} for e in self.eng}
        self.cur = {}
        self.epoch = {}
        for e in self.eng:
            self.epoch[e] = 0
            self.cur[e] = f"{e}#0"
            self.newsem(self.cur[e])
        self.nops = {e: 0 for e in self.eng}

    def newsem(self, key):
        self.sems[key] = self.es.enter_context(self.nc.semaphore("s_" + key))
        self.cnt[key] = 0

    def _wait(self, e, reads, writes):
        need = {}
        for b in reads:
            if b.w is not None:
                k, c = b.w
                if need.get(k, 0) < c:
                    need[k] = c
            if b.excl:
                for k, c in b.r.items():
                    if k.split("#")[0] != e and need.get(k, 0) < c:
                        need[k] = c
        for b in writes:
            if b.w is not None:
                k, c = b.w
                if need.get(k, 0) < c:
                    need[k] = c
            for k, c in b.r.items():
                if need.get(k, 0) < c:
                    need[k] = c
        kn = self.known[e]
        for k, c in need.items():
            if kn.get(k, 0) < c:
                self.eng[e].wait_ge(self.sems[k], c)
                kn[k] = c

    def _record(self, ev, reads, writes):
        k, c = ev
        for b in reads:
            if not b.const:
                if b.r.get(k, 0) < c:
                    b.r[k] = c
        for b in writes:
            b.w = ev
            b.r = {}

    def op(self, e, fn, reads=(), writes=()):
        self._wait(e, reads, writes)
        ins = fn(self.eng[e])
        key = self.cur[e]
        if self.cnt[key] >= 2000:
            self.epoch[e] += 1
            key = f"{e}#{self.epoch[e]}"
            self.cur[e] = key
            self.newsem(key)
        self.cnt[key] += 1
        ins.then_inc(self.sems[key], 1)
        ev = (key, self.cnt[key])
        self._record(ev, reads, writes)
        self.nops[e] += 1
        return ev

    def dma(self, q, out, in_, key, reads=(), writes=(), chain=True):
        self._wait(q, reads, writes)
        if chain and self.cnt[key] > 0 and self.known[q].get(key, 0) < self.cnt[key]:
            self.eng[q].wait_ge(self.sems[key], self.cnt[key])
            self.known[q][key] = self.cnt[key]
        ins = self.eng[q].dma_start(out=out, in_=in_)
        self.cnt[key] += 16
        ins.then_inc(self.sems[key], 16)
        ev = (key, self.cnt[key])
        self._record(ev, reads, writes)
        return ev


def build(n_layers=4, dbg=False):
    nc = bass.Bass("TRN2", target_bir_lowering=False)
    es = ExitStack()
    with es:
        K = Kern(nc, es)
        _build(nc, es, K, n_layers, dbg)
    return nc


def _build(nc, es, K, n_layers, dbg):
    def din(name, shape, dt=F32):
        return nc.dram_tensor(name, list(shape), dt, kind="ExternalInput").ap()

    def dout(name, shape, dt=F32):
        return nc.dram_tensor(name, list(shape), dt, kind="ExternalOutput").ap()

    xp = din("xp", [HALF, D])
    xo = din("xo", [HALF, D])
    memx = din("memx", [256, D])
    w_in_a = din("w_in_a", [2, D, 3328])
    w_out_a = din("w_out_a", [2, D, D])
    w_in_b = din("w_in_b", [2, D, D])
    w_out_b = din("w_out_b", [2, D, D])
    w_mem_kv = din("w_mem_kv", [4, D, 512])
    w_kv = din("w_kv", [D, 512])
    w_up = din("w_up", [4, D, 4096])
    w_down = din("w_down", [4, 4096, D])
    gains_d = din("gains", [128, 14 * 8])
    gfin_d = din("gfin", [128, D])
    sinks_d = din("sinks", [128, 24])
    ident_d = din("ident", [128, 128], BF16)
    maskT_d = din("maskT", [128, 6 * 128])
    wq_d = din("wq", [128, 6 * 128])
    wk_d = din("wk", [128, 6])
    dcur_d = din("dcur", [128, 128], BF16)
    dprev_d = din("dprev", [128, 128], BF16)
    flags_d = din("flags", [128, 2])

    xs_d = din("xs", [128, D])
    state_s = din("state_s", [2, 16, 6, 128, 128])
    memk_s = din("memk_s", [4, 16, 256, 256])
    memv_s = din("memv_s", [4, 16, 256, 256])
    swak_s = din("swak_s", [16, 128, 256])
    swav_s = din("swav_s", [16, 128, 256])
    eyecol_d = din("eyecol", [128, 16])
    eye16_d = din("eye16", [128, 256])
    sel_d = din("sel", [128, 16 * 128], BF16)
    ab_d = din("ab", [128, 12])
    ys_d = dout("ys", [16, D])
    rets_d = dout("rets", [2, 16, 6, 128, 128])
    swako_d = dout("swako", [16, 128, 256])
    swavo_d = dout("swavo", [16, 128, 256])
    y_d = dout("y", [HALF, D])
    ret_d = dout("ret", [2, 6, 128, 128])
    swak_d = dout("swak", [128, 256])
    swav_d = dout("swav", [128, 256])
    memk_d = dout("memk", [4, 256, 256])
    memv_d = dout("memv", [4, 256, 256])

    def sb(name, shape, dt):
        return es.enter_context(nc.sbuf_tensor("sb_" + name, list(shape), dt))

    def ps(name, shape, dt):
        return es.enter_context(nc.psum_tensor("ps_" + name, list(shape), dt))

    xres = sb("xres", [128, NCG, D], F32)
    xres_b = [Buf(f"x{c}") for c in range(NCG)]
    ring = [sb(f"ring{i}", [128, 2048], BF16) for i in range(NSLOT)]
    ring_b = [Buf(f"ring{i}") for i in range(NSLOT)]
    for i in range(NSLOT):
        K.newsem(f"ring{i}")
    ring_pos = [0]

    gains = sb("gains", [128, 14, 8], F32)
    sinks = sb("sinks", [128, 24], F32)
    esink = sb("esink", [128, 24], F32)
    ident = sb("ident", [128, 128], BF16)
    maskT = sb("maskT", [128, 6, 128], F32)
    wq = sb("wq", [128, 6, 128], F32)
    wk = sb("wk", [128, 6], F32)
    dcur = sb("dcur", [128, 128], BF16)
    dprev = sb("dprev", [128, 128], BF16)
    flags = sb("flags", [128, 2], F32)
    epsc = sb("epsc", [128, 1], F32)
    constb = Buf("const", const=True)
    K.newsem("cload")

    S_b = [[Buf(f"S{l}_{h}") for h in range(6)] for l in range(2)]
    Sb_b = [[[Buf(f"Sb{l}_{p}_{h}") for h in range(6)] for p in range(2)] for l in range(2)]
    mk_b = Buf("mkT")
    mv_b = Buf("mv1")

    xs = [sb(f"xs{j}", [128, D], BF16) for j in range(2)]
    xs_b = [Buf(f"xs{j}") for j in range(2)]
    stat = sb("stat", [128, 64], F32)
    stat_b = [Buf(f"stat{j}") for j in range(2)]
    xnTs = [sb(f"xnT{p}", [128, 8, TS], BF16) for p in range(2)]
    xnT_bs = [[Buf(f"xnT{p}_{j}") for j in range(CH)] for p in range(2)]
    xnT = xnTs[0]
    xnT_b = xnT_bs[0]

    def set_xn(p):
        nonlocal xnT, xnT_b
        xnT = xnTs[p]
        xnT_b = xnT_bs[p]
    WK = sb("wkall", [128, 9216], BF16)
    qT = WK[:, 0:1536].rearrange("p (a b) -> p a b", a=6)
    qT_b = Buf("qT")
    qwT = WK[:, 1536:3072].rearrange("p (a b) -> p a b", a=6)
    qwT_b = Buf("qwT")
    kT = WK[:, 3072:4608].rearrange("p (a b) -> p a b", a=6)
    kT_b = Buf("kT")
    qmT = sb("qmT", [128, 2, TS], BF16)
    qmT_b = Buf("qmT")
    tmA = WK[:, 4608:9216].rearrange("p (a b) -> p a b", a=6)
    ktm = [tmA[:, j, :] for j in range(CH)]
    ktm_b = [Buf(f"ktm{j}") for j in range(CH)]
    vtm = [tmA[:, 2 + j, :] for j in range(CH)]
    vtm_b = [Buf(f"vtm{j}") for j in range(CH)]
    gtm = [tmA[:, 4 + j, :] for j in range(CH)]
    gtm_b = [Buf(f"gtm{j}") for j in range(CH)]
    WKB = [qT_b, qwT_b, kT_b] + ktm_b + vtm_b + gtm_b
    scm = sb("scm", [128, 6, 128], BF16)
    scm_b = [Buf("scm_a"), Buf("scm_b")]
    osb = sb("osb", [128, 6, 128], F32)
    osb_b = Buf("osb")
    osq = sb("osq", [128, 6, 128], F32)
    osq_b = Buf("osq")
    gst = sb("gst", [128, 48], F32)
    gst_b = Buf("gst")
    gst2 = sb("gst2", [128, 8], F32)
    gst2_b = Buf("gst2")
    eT = sb("eT", [128, 4, 2, TS], BF16)
    eT_b = [Buf(f"eT{h}") for h in range(4)]
    rden = sb("rden", [128, 16], F32)
    rden_b = Buf("rden")
    cat = [sb(f"cat{j}", [128, D], BF16) for j in range(2)]
    cat_b = [Buf(f"cat{j}") for j in range(2)]
    catT = sb("catT", [128, 8, 128], BF16)
    catT_b = Buf("catT")
    hT = sb("hT", [128, 16, TS], BF16)
    hT_b = [Buf(f"hT{i}") for i in range(16)]
    hrs = [sb(f"hr{i}", [128, TS], BF16) for i in range(2)]
    hrs_b = [Buf(f"hr{i}") for i in range(2)]
    xnall = WK[:, 0:8192].rearrange("p (s k t) -> p s k t", s=NST, k=8)
    xnall_b = [[Buf(f"xnall{s}_{j}") for j in range(CH)] for s in range(NST)]
    yst0 = sb("yst0", [128, D], F32)
    yst = [yst0, yst0]
    yb0 = Buf("yst0")
    yst_b = [yb0, yb0]
    for j in range(2):
        K.newsem(f"yst{j}")
    K.newsem("xload")
    K.newsem("misc_out")
    memx_sb = xres[:, 0:2, :]
    memx_b = xres_b[0]
    mst = yst0[:, 0:512]
    mst_b = yb0
    memnT = xnTs[0]
    memnT_b = xnT_bs[0][0]
    mrstd = sb("mrstd", [128, 8], F32)
    mrstd_b = Buf("mrstd")

    kshT_b = [Buf(f"kshT{c}") for c in range(NCG + 1)]
    vsh_b = [Buf(f"vsh{c}") for c in range(NCG + 1)]
    kvout = yst0[:, 512:1024]
    kvout_b = yb0
    qsT = qT
    qsT_b = qT_b
    st_t = osb[:, 0:4, :]
    st_b = [osb_b, osq_b]
    st_t2 = osq[:, 0:4, :]
    esT = WK[:, 4608:4608 + 3072].rearrange("p (a b c) -> p a b c", a=12, b=2)
    esT_b = [Buf(f"esT{i}") for i in range(6)]

    NPB = 8
    pbank = [ps(f"pb{i}", [128, 512], F32) for i in range(NPB)]
    pbank_b = [Buf(f"pb{i}", excl=True) for i in range(NPB)]
    pbi = [0]

    held = set()

    def nextpb():
        while True:
            i = pbi[0]
            pbi[0] = (i + 1) % NPB
            if i not in held:
                return pbank[i], pbank_b[i]

    def holdpb():
        while True:
            i = pbi[0]
            pbi[0] = (i + 1) % NPB
            if i not in held:
                held.add(i)
                return pbank[i], pbank_b[i], i

    def nextpt():
        pb, pbb = nextpb()
        return pb[:].bitcast(BF16).rearrange("p (a b) -> p a b", a=8), pbb

    def cload(dst, src):
        K.dma("sp", dst, src, "cload", writes=[constb], chain=False)

    cload(gains[:].rearrange("p a b -> p (a b)"), gains_d)
    cload(sinks[:], sinks_d)
    cload(ident[:], ident_d)
    cload(maskT[:].rearrange("p a b -> p (a b)"), maskT_d)
    cload(wq[:].rearrange("p a b -> p (a b)"), wq_d)
    cload(wk[:], wk_d)
    cload(dcur[:], dcur_d)
    cload(dprev[:], dprev_d)
    cload(flags[:], flags_d)
    cev = ("cload", K.cnt["cload"])
    for e in ("pe", "act", "dve"):
        K.eng[e].wait_ge(K.sems["cload"], cev[1])
        K.known[e]["cload"] = cev[1]
    constb.w = None
    initb = Buf("init")
    K.op("dve", lambda v: v.memset(epsc[:], EPS), writes=[initb])
    K.op("act", lambda a: a.activation(out=esink[:], in_=sinks[:], func=AF.Exp), reads=[initb], writes=[initb])
    for e in ("pe", "act", "dve"):
        K._wait(e, [initb], [])

    GI = {"mix": 0, "mlp": 4, "mem": 8, "kv": 12, "fin": 13}
    st_b4 = [[Buf(f"st{hh}_{i}") for i in range(4)] for hh in range(2)]
    ALLWK = WKB + esT_b + [b for row in xnall_b for b in row] + [osb_b, osq_b] + st_b4[0] + st_b4[1]

    def fence():
        for e in ("pe", "act", "dve"):
            K._wait(e, [], ALLWK)

    def load_piece(src_ap, shape):
        i = ring_pos[0]
        ring_pos[0] = (i + 1) % NSLOT
        a, b = shape
        dst = ring[i][:].rearrange("p (a b) -> p a b", a=a)
        K.dma("pool", dst, src_ap, f"ring{i}", writes=[ring_b[i]])
        return i

    def piece_cols(w2d, c0, ncol=256):
        return load_piece(w2d.rearrange("(k p) c -> p k c", p=128)[:, :, c0:c0 + ncol], (8, ncol))

    def piece_rows(w2d, r0, c0):
        return load_piece(w2d[r0:r0 + 512, c0:c0 + 512].rearrange("(k p) c -> p k c", p=128), (4, 512))

    def rv8(i):
        return ring[i][:].rearrange("p (a b) -> p a b", a=8)

    def rv4(i):
        return ring[i][:].rearrange("p (a b) -> p a b", a=4)

    def rms_stats(x_ap, xb, j):
        o = j * 8
        K.op("act", lambda a: a.activation(out=xs[j][:], in_=x_ap, func=AF.Square, scale=1.0 / 32.0,
                                           accum_out=stat[:, o:o + 1]),
             reads=[xb], writes=[xs_b[j], stat_b[j]])
        K.op("act", lambda a: a.activation(out=stat[:, o + 1:o + 2], in_=stat[:, o:o + 1], func=AF.Ln,
                                           bias=epsc[:, 0:1], scale=1.0),
             reads=[stat_b[j]], writes=[stat_b[j]])
        K.op("act", lambda a: a.activation(out=stat[:, o + 2:o + 3], in_=stat[:, o + 1:o + 2], func=AF.Exp,
                                           scale=-0.5),
             reads=[stat_b[j]], writes=[stat_b[j]])
        return stat[:, o + 2:o + 3]

    def norm_A(x_ap, xb, j):
        rstd = rms_stats(x_ap, xb, j)
        K.op("act", lambda a: a.activation(out=xs[j][:], in_=x_ap, func=AF.Copy, scale=rstd),
             reads=[xb, stat_b[j]], writes=[xs_b[j]])

    def norm_B(j, gi, dstT, dst_b):
        pt, ptbuf = nextpt()

        def tr(pe):
            for k in range(8):
                ins = pe.transpose(out=pt[:, k, :], in_=xs[j][:, k * 128:(k + 1) * 128], identity=ident[:])
            return ins
        K.op("pe", tr, reads=[xs_b[j]], writes=[ptbuf])
        K.op("dve", lambda v: v.tensor_tensor(out=dstT, in0=pt[:], in1=gains[:, gi, :].unsqueeze(2).to_broadcast([128, 8, 128]),
                                              op=ALU.mult),
             reads=[ptbuf], writes=[dst_b])

    def norm_T(x_ap, xb, j, gi, dstT, dst_b):
        norm_A(x_ap, xb, j)
        norm_B(j, gi, dstT, dst_b)

    def st_norm_A(st):
        for j in range(CH):
            c = st * CH + j
            norm_A(xres[:, c, :], xres_b[c], j)

    def st_norm_B(gi, p):
        for j in range(CH):
            norm_B(j, gi, xnTs[p][:, :, j * 128:(j + 1) * 128], xnT_bs[p][j])

    def mm_group(pe, out_ap, lhs_fn, rhs_fn, nk):
        for k in range(nk):
            ins = pe.matmul(out_ap, lhsT=lhs_fn(k), rhs=rhs_fn(k), start=(k == 0), stop=(k == nk - 1))
        return ins

    def mem_prologue():
        for mc in range(2):
            rstd = rms_stats(memx_sb[:, mc, :], memx_b, mc)
            K.op("dve", lambda v, mc=mc, rstd=rstd: v.tensor_copy(out=mrstd[:, mc:mc + 1], in_=rstd),
                 reads=[stat_b[mc]], writes=[mrstd_b])
        KSTOP = 99
        if KSTOP < 1:
            return
        for l in range(4):
            if KSTOP < 2 and l > 0:
                return
            p0 = piece_cols(w_mem_kv[l], 0)
            p1 = piece_cols(w_mem_kv[l], 256)
            for mc in range(2):
                K.op("act", lambda a, mc=mc: a.activation(out=xs[mc][:], in_=memx_sb[:, mc, :], func=AF.Copy,
                                                          scale=mrstd[:, mc:mc + 1]),
                     reads=[memx_b, mrstd_b], writes=[xs_b[mc]])
                pt, ptbuf = nextpt()

                def tr(pe, mc=mc, pt=pt):
                    for k in range(8):
                        ins = pe.transpose(out=pt[:, k, :], in_=xs[mc][:, k * 128:(k + 1) * 128], identity=ident[:])
                    return ins
                K.op("pe", tr, reads=[xs_b[mc]], writes=[ptbuf])
                K.op("dve", lambda v, mc=mc, pt=pt, l=l: v.tensor_tensor(
                    out=memnT[:, :, mc * 128:(mc + 1) * 128], in0=pt[:],
                    in1=gains[:, GI["mem"] + l, :].unsqueeze(2).to_broadcast([128, 8, 128]), op=ALU.mult),
                    reads=[ptbuf], writes=[memnT_b])
            if KSTOP < 3:
                continue
            for blk in range(2):
                pb, pbb = nextpb()
                K.op("pe", lambda pe, pb=pb, blk=blk: mm_group(
                    pe, pb[:, 0:256], lambda k: rv8(p0)[:, k, blk * 128:(blk + 1) * 128],
                    lambda k: memnT[:, k, :], 8),
                    reads=[ring_b[p0], memnT_b], writes=[pbb])
                K.op("act", lambda a, pb=pb, blk=blk, l=l: a.activation(out=mkT[:, l, blk, :], in_=pb[:, 0:256], func=AF.Copy),
                     reads=[pbb], writes=[mk_b])
            if KSTOP < 4:
                continue
            for mc in range(2):
                pb, pbb = nextpb()

                def tm(pe, pb=pb, mc=mc):
                    mm_group(pe, pb[:, 0:256], lambda k: memnT[:, k, mc * 128:(mc + 1) * 128], lambda k: rv8(p0)[:, k, :], 8)
                    return mm_group(pe, pb[:, 256:512], lambda k: memnT[:, k, mc * 128:(mc + 1) * 128], lambda k: rv8(p1)[:, k, :], 8)
                K.op("pe", tm, reads=[ring_b[p0], ring_b[p1], memnT_b], writes=[pbb])
                K.op("act", lambda a, pb=pb: a.activation(out=mst[:], in_=pb[:], func=AF.Copy), reads=[pbb], writes=[mst_b])
                if KSTOP != 5:
                    K.op("dve", lambda v, pb=pb, mc=mc, l=l: v.tensor_copy(
                        out=mv1[:, l, mc, :, 0:64], in_=pb[:, 256:512].rearrange("p (h d) -> p h d", h=4)),
                        reads=[pbb], writes=[mv_b])
                if KSTOP != 6:
                    K.dma("sp", memk_d[l, mc * 128:(mc + 1) * 128, :], mst[:, 0:256], "misc_out", reads=[mst_b])
                    K.dma("sp", memv_d[l, mc * 128:(mc + 1) * 128, :], mst[:, 256:512], "misc_out", reads=[mst_b])

    def state_update(l, j):
        pKa, pKab = nextpb()
        pKb, pKbb = nextpb()

        def kvm(pe, pk, hs):
            for i, h in enumerate(hs):
                ins = pe.matmul(pk[:, i * 128:(i + 1) * 128], lhsT=ktm[j][:, h * 128:(h + 1) * 128], rhs=vtm[j][:, h * 128:(h + 1) * 128], start=True, stop=True)
            return ins
        K.op("pe", lambda pe: kvm(pe, pKa, (0, 1, 2, 3)), reads=[ktm_b[j], vtm_b[j]], writes=[pKab])
        K.op("pe", lambda pe: kvm(pe, pKb, (4, 5)), reads=[ktm_b[j], vtm_b[j]], writes=[pKbb])
        for h in range(6):
            pk, pkb, i = (pKa, pKab, h) if h < 4 else (pKb, pKbb, h - 4)
            K.op("dve", lambda v, pk=pk, i=i, h=h: v.scalar_tensor_tensor(
                out=S[l][:, h, :], in0=S[l][:, h, :], scalar=GC[h], in1=pk[:, i * 128:(i + 1) * 128],
                op0=ALU.mult, op1=ALU.add),
                reads=[pkb, S_b[l][h]], writes=[S_b[l][h]])
            K.op("act", lambda a, h=h: a.activation(out=Sb[l][(j + 1) % 2][:, h, :], in_=S[l][:, h, :], func=AF.Copy),
                 reads=[S_b[l][h]], writes=[Sb_b[l][(j + 1) % 2][h]])

    def mixer_a(l, full_sts=None):
        fence()
        sts = list(range(NST))
        if full_sts is None:
            full_sts = set(sts)
        wi = w_in_a[l]
        if full_sts:
            pin = [piece_cols(wi, c * 256) for c in range(13)]
            wo = w_out_a[l]
            pout = [[piece_rows(wo, rg * 512, ch * 512) for rg in range(2)] for ch in range(2)]
        else:
            pin = {c: piece_cols(wi, c * 256) for c in range(3, 9)}
            pout = None
        gi = GI["mix"] + l
        pcur = 0
        st_norm_A(sts[0])
        st_norm_B(gi, pcur)
        set_xn(pcur)
        a_inproj_main(l, sts[0], sts[0] in full_sts, pin)
        if sts[0] in full_sts:
            a_inproj_gate(l, sts[0], pin)
        for idx, st in enumerate(sts):
            full = st in full_sts
            nxt = sts[idx + 1] if idx + 1 < len(sts) else None
            if nxt is not None:
                st_norm_A(nxt)
            pnext = 1 - pcur
            a_body(l, st, full, mid=(lambda: st_norm_B(gi, pnext)) if nxt is not None else None)
            if nxt is not None:
                pcur = pnext
                set_xn(pcur)
                a_inproj_main(l, nxt, nxt in full_sts, pin)
            if full:
                for j in range(CH):
                    out_proj(st * CH + j, cat[j], cat_b[j], pout)
            if nxt is not None and nxt in full_sts:
                a_inproj_gate(l, nxt, pin)
        set_xn(0)

    def a_inproj_main(l, st, full, pin):
        if True:
            for blk in (range(14) if full else ()):
                col = blk * 128 if blk < 12 else 3072 + (blk - 12) * 128
                pi = pin[col // 256]
                co = col % 256
                pb, pbb = nextpb()
                K.op("pe", lambda pe, pb=pb, pi=pi, co=co: mm_group(
                    pe, pb[:, 0:TS], lambda k: rv8(pi)[:, k, co:co + 128], lambda k: xnT[:, k, :], 8),
                    reads=[ring_b[pi]] + xnT_b, writes=[pbb])
                if blk < 6:
                    h = blk
                    K.op("act", lambda a, pb=pb, h=h: a.activation(out=qT[:, h, :], in_=pb[:, 0:TS], func=AF.Copy),
                         reads=[pbb], writes=[qT_b])
                    K.op("dve", lambda v, pb=pb, h=h: v.tensor_tensor(
                        out=qwT[:, h, :].rearrange("p (c q) -> p c q", c=CH),
                        in0=pb[:, 0:TS].rearrange("p (c q) -> p c q", c=CH),
                        in1=wq[:, h, :].unsqueeze(1).to_broadcast([128, CH, 128]), op=ALU.mult),
                        reads=[pbb], writes=[qwT_b])
                elif blk < 12:
                    h = blk - 6
                    K.op("act", lambda a, pb=pb, h=h: a.activation(out=kT[:, h, :], in_=pb[:, 0:TS], func=AF.Copy),
                         reads=[pbb], writes=[kT_b])
                else:
                    b2 = blk - 12
                    K.op("act", lambda a, pb=pb, b2=b2: a.activation(out=qmT[:, b2, :], in_=pb[:, 0:TS], func=AF.Copy),
                         reads=[pbb], writes=[qmT_b])
            for j in range(CH):
                for pp in range(3, 9):
                    pi = pin[pp]
                    pb, pbb = nextpb()
                    K.op("pe", lambda pe, pb=pb, pi=pi, j=j: mm_group(
                        pe, pb[:, 0:256], lambda k: xnT[:, k, j * 128:(j + 1) * 128], lambda k: rv8(pi)[:, k, :], 8),
                        reads=[ring_b[pi], xnT_b[j]], writes=[pbb])
                    cc = (pp - 3) % 3 * 256
                    if pp < 6:
                        h0 = cc // 128
                        K.op("dve", lambda v, pb=pb, j=j, cc=cc, h0=h0: v.tensor_tensor(
                            out=ktm[j][:, cc:cc + 256].rearrange("p (h d) -> p h d", h=2),
                            in0=pb[:, 0:256].rearrange("p (h d) -> p h d", h=2),
                            in1=wk[:, h0:h0 + 2].unsqueeze(2).to_broadcast([128, 2, 128]), op=ALU.mult),
                            reads=[pbb], writes=[ktm_b[j]])
                    elif pp < 9:
                        K.op("act", lambda a, pb=pb, j=j, cc=cc: a.activation(out=vtm[j][:, cc:cc + 256], in_=pb[:, 0:256], func=AF.Copy),
                             reads=[pbb], writes=[vtm_b[j]])
                    else:
                        K.op("act", lambda a, pb=pb, j=j, cc=cc: a.activation(out=gtm[j][:, cc:cc + 256], in_=pb[:, 0:256], func=AF.Silu),
                             reads=[pbb], writes=[gtm_b[j]])

    def a_inproj_gate(l, st, pin):
        if True:
            for j in range(CH):
                for pp in range(9, 12):
                    pi = pin[pp]
                    pb, pbb = nextpb()
                    K.op("pe", lambda pe, pb=pb, pi=pi, j=j: mm_group(
                        pe, pb[:, 0:256], lambda k: xnT[:, k, j * 128:(j + 1) * 128], lambda k: rv8(pi)[:, k, :], 8),
                        reads=[ring_b[pi], xnT_b[j]], writes=[pbb])
                    cc = (pp - 3) % 3 * 256
                    if pp < 6:
                        h0 = cc // 128
                        K.op("dve", lambda v, pb=pb, j=j, cc=cc, h0=h0: v.tensor_tensor(
                            out=ktm[j][:, cc:cc + 256].rearrange("p (h d) -> p h d", h=2),
                            in0=pb[:, 0:256].rearrange("p (h d) -> p h d", h=2),
                            in1=wk[:, h0:h0 + 2].unsqueeze(2).to_broadcast([128, 2, 128]), op=ALU.mult),
                            reads=[pbb], writes=[ktm_b[j]])
                    elif pp < 9:
                        K.op("act", lambda a, pb=pb, j=j, cc=cc: a.activation(out=vtm[j][:, cc:cc + 256], in_=pb[:, 0:256], func=AF.Copy),
                             reads=[pbb], writes=[vtm_b[j]])
                    else:
                        K.op("act", lambda a, pb=pb, j=j, cc=cc: a.activation(out=gtm[j][:, cc:cc + 256], in_=pb[:, 0:256], func=AF.Silu),
                             reads=[pbb], writes=[gtm_b[j]])

    def a_body(l, st, full, mid=None):
        if True:
            for hm in (range(4) if full else ()):
                pb, pbb = nextpb()
                lo = (hm % 2) * 64

                def sc(pe, pb=pb, hm=hm, lo=lo):
                    for mc in range(2):
                        ins = pe.matmul(pb[:, mc * TS:(mc + 1) * TS], lhsT=mkT[lo:lo + 64, l, hm // 2, mc * 128:(mc + 1) * 128],
                                        rhs=qmT[lo:lo + 64, hm // 2, :], start=True, stop=True)
                    return ins
                K.op("pe", sc, reads=[mk_b, qmT_b], writes=[pbb])
                K.op("act", lambda a, pb=pb, hm=hm: a.activation(out=eT[:, hm, :, :].rearrange("p a b -> p (a b)"), in_=pb[:], func=AF.Exp, scale=0.125),
                     reads=[pbb], writes=[eT_b[hm]])
            pOs = {}
            for j in range(CH):
                js = slice(j * 128, (j + 1) * 128)
                state_update(l, j)
                if not full:
                    continue
                pA, pAb = nextpb()
                pB, pBb = nextpb()

                def scr(pe, pA=pA, hs=(0, 1, 2, 3), js=js):
                    for i, h in enumerate(hs):
                        ins = pe.matmul(pA[:, i * 128:(i + 1) * 128], lhsT=kT[:, h, js], rhs=qT[:, h, js], start=True, stop=True)
                    return ins
                K.op("pe", scr, reads=[kT_b, qT_b], writes=[pAb])
                K.op("pe", lambda pe, pB=pB, js=js: scr(pe, pB, (4, 5), js), reads=[kT_b, qT_b], writes=[pBb])
                K.op("dve", lambda v, pA=pA: v.tensor_tensor(out=scm[:, 0:4, :], in0=pA[:].rearrange("p (h q) -> p h q", h=4),
                                                             in1=maskT[:, 0:4, :], op=ALU.mult),
                     reads=[pAb], writes=[scm_b[0]])
                K.op("dve", lambda v, pB=pB: v.tensor_tensor(out=scm[:, 4:6, :], in0=pB[:, 0:256].rearrange("p (h q) -> p h q", h=2),
                                                             in1=maskT[:, 4:6, :], op=ALU.mult),
                     reads=[pBb], writes=[scm_b[1]])
                pOa, pOab, ia = holdpb()
                pOb, pObb, ib = holdpb()

                def omm(pe, po, hs, j=j, js=js):
                    for i, h in enumerate(hs):
                        pe.matmul(po[:, i * 128:(i + 1) * 128], lhsT=scm[:, h, :], rhs=vtm[j][:, h * 128:(h + 1) * 128], start=True, stop=False)
                        ins = pe.matmul(po[:, i * 128:(i + 1) * 128], lhsT=qwT[:, h, js], rhs=Sb[l][j % 2][:, h, :], start=False, stop=True)
                    return ins
                K.op("pe", lambda pe, pOa=pOa: omm(pe, pOa, (0, 1, 2, 3)), reads=[scm_b[0], vtm_b[j], qwT_b] + Sb_b[l][j % 2][0:4], writes=[pOab])
                K.op("pe", lambda pe, pOb=pOb: omm(pe, pOb, (4, 5)), reads=[scm_b[1], vtm_b[j], qwT_b] + Sb_b[l][j % 2][4:6], writes=[pObb])
                pOs[j] = (pOa, pOab, pOb, pObb, ia, ib)
            if mid is not None:
                mid()
            for j in (range(CH) if full else ()):
                js = slice(j * 128, (j + 1) * 128)
                cb = cat[j]
                cbb = cat_b[j]
                pOa, pOab, pOb, pObb, ia, ib = pOs[j]
                gn_gate(pOa, pOab, pOb, pObb, cb, cbb, gtm[j], gtm_b[j], 128)
                held.discard(ia)
                held.discard(ib)
                pM, pMb = nextpb()

                def mo(pe, pM=pM, js=js):
                    for hm in range(4):
                        for mc in range(2):
                            ins = pe.matmul(pM[:, hm * 65:(hm + 1) * 65], lhsT=eT[:, hm, mc, js], rhs=mv1[:, l, mc, hm, :],
                                            start=(mc == 0), stop=(mc == 1))
                    return ins
                K.op("pe", mo, reads=eT_b + [mv_b], writes=[pMb])
                mem_out(pM, pMb, cb, cbb)

    def gn_full(pOa, pOab, pOb, pObb, cb, cbb, gate, gate_b):
        K.op("dve", lambda v: v.tensor_reduce(out=gst[:, 0:4], in_=pOa[:].rearrange("p (h e) -> p h e", h=4), axis=AX.X, op=ALU.add),
             reads=[pOab], writes=[gst_b])
        K.op("dve", lambda v: v.tensor_reduce(out=gst[:, 4:6], in_=pOb[:, 0:256].rearrange("p (h e) -> p h e", h=2), axis=AX.X, op=ALU.add),
             reads=[pObb], writes=[gst_b])
        for h in range(6):
            po, pob, i = (pOa, pOab, h) if h < 4 else (pOb, pObb, h - 4)
            K.op("act", lambda a, po=po, i=i, h=h: a.activation(out=hrs[0][:, 0:128], in_=po[:, i * 128:(i + 1) * 128], func=AF.Square,
                                                              accum_out=gst2[:, h:h + 1]),
                 reads=[pob], writes=[hrs_b[0], gst2_b])
        K.op("dve", lambda v: v.tensor_scalar(out=gst[:, 12:18], in0=gst[:, 0:6], scalar1=1.0 / 128.0, scalar2=None, op0=ALU.mult),
             reads=[gst_b], writes=[gst_b])
        K.op("dve", lambda v: v.tensor_tensor(out=gst[:, 18:24], in0=gst[:, 12:18], in1=gst[:, 12:18], op=ALU.mult),
             reads=[gst_b], writes=[gst_b])
        K.op("dve", lambda v: v.scalar_tensor_tensor(out=gst[:, 24:30], in0=gst2[:, 0:6], scalar=1.0 / 128.0, in1=gst[:, 18:24],
                                                      op0=ALU.mult, op1=ALU.subtract),
             reads=[gst_b, gst2_b], writes=[gst_b])
        K.op("act", lambda a: a.activation(out=gst[:, 30:36], in_=gst[:, 24:30], func=AF.Ln, bias=epsc[:, 0:1], scale=1.0),
             reads=[gst_b], writes=[gst_b])
        K.op("act", lambda a: a.activation(out=gst[:, 36:42], in_=gst[:, 30:36], func=AF.Exp, scale=-0.5),
             reads=[gst_b], writes=[gst_b])
        K.op("dve", lambda v: v.scalar_tensor_tensor(out=gst[:, 42:48], in0=gst[:, 12:18], scalar=-1.0, in1=gst[:, 36:42],
                                                      op0=ALU.mult, op1=ALU.mult),
             reads=[gst_b], writes=[gst_b])
        for h in range(6):
            po, pob, i = (pOa, pOab, h) if h < 4 else (pOb, pObb, h - 4)
            K.op("act", lambda a, po=po, i=i, h=h: a.activation(out=cb[:, h * 128:(h + 1) * 128], in_=po[:, i * 128:(i + 1) * 128], func=AF.Identity,
                                                              scale=gst[:, 36 + h:37 + h], bias=gst[:, 42 + h:43 + h]),
                 reads=[pob, gst_b], writes=[cbb])
        K.op("dve", lambda v: v.tensor_tensor(out=cb[:, 0:MIX], in0=cb[:, 0:MIX], in1=gate[:, :], op=ALU.mult),
             reads=[cbb, gate_b], writes=[cbb])

    def gn_gate(pOa, pOab, pOb, pObb, cb, cbb, gate, gate_b, P):
        K.op("act", lambda a: a.activation(out=osb[0:P, 0:4, :].rearrange("p a b -> p (a b)"), in_=pOa[0:P, :], func=AF.Copy),
             reads=[pOab], writes=[osb_b])
        K.op("act", lambda a: a.activation(out=osb[0:P, 4:6, :].rearrange("p a b -> p (a b)"), in_=pOb[0:P, 0:256], func=AF.Copy),
             reads=[pObb, osb_b], writes=[osb_b])
        K.op("dve", lambda v: v.tensor_reduce(out=gst[0:P, 0:6], in_=osb[0:P], axis=AX.X, op=ALU.add),
             reads=[osb_b], writes=[gst_b])
        K.op("dve", lambda v: v.tensor_tensor(out=osq[0:P], in0=osb[0:P], in1=osb[0:P], op=ALU.mult),
             reads=[osb_b], writes=[osq_b])
        K.op("dve", lambda v: v.tensor_reduce(out=gst[0:P, 6:12], in_=osq[0:P], axis=AX.X, op=ALU.add),
             reads=[osq_b, gst_b], writes=[gst_b])
        K.op("dve", lambda v: v.tensor_scalar(out=gst[0:P, 12:18], in0=gst[0:P, 0:6], scalar1=1.0 / 128.0, scalar2=None, op0=ALU.mult),
             reads=[gst_b], writes=[gst_b])
        K.op("dve", lambda v: v.tensor_tensor(out=gst[0:P, 18:24], in0=gst[0:P, 12:18], in1=gst[0:P, 12:18], op=ALU.mult),
             reads=[gst_b], writes=[gst_b])
        K.op("dve", lambda v: v.scalar_tensor_tensor(out=gst[0:P, 24:30], in0=gst[0:P, 6:12], scalar=1.0 / 128.0, in1=gst[0:P, 18:24],
                                                      op0=ALU.mult, op1=ALU.subtract),
             reads=[gst_b], writes=[gst_b])
        K.op("act", lambda a: a.activation(out=gst[0:P, 30:36], in_=gst[0:P, 24:30], func=AF.Ln, bias=epsc[0:P, 0:1], scale=1.0),
             reads=[gst_b], writes=[gst_b])
        K.op("act", lambda a: a.activation(out=gst[0:P, 36:42], in_=gst[0:P, 30:36], func=AF.Exp, scale=-0.5),
             reads=[gst_b], writes=[gst_b])
        K.op("dve", lambda v: v.tensor_tensor(out=osb[0:P], in0=osb[0:P], in1=gst[0:P, 12:18].unsqueeze(2).to_broadcast([P, 6, 128]), op=ALU.subtract),
             reads=[osb_b, gst_b], writes=[osb_b])
        K.op("dve", lambda v: v.tensor_tensor(out=osb[0:P], in0=osb[0:P], in1=gst[0:P, 36:42].unsqueeze(2).to_broadcast([P, 6, 128]), op=ALU.mult),
             reads=[osb_b, gst_b], writes=[osb_b])
        K.op("dve", lambda v: v.tensor_tensor(out=cb[0:P, 0:MIX], in0=osb[0:P].rearrange("p a b -> p (a b)"), in1=gate[0:P, :], op=ALU.mult),
             reads=[osb_b, gate_b], writes=[cbb])

    def mem_out(pM, pMb, cb, cbb, P=128):
        pv = pM[0:P, 0:260].rearrange("p (h d) -> p h d", h=4)
        K.op("dve", lambda v: v.reciprocal(out=rden[0:P, 0:4], in_=pv[:, :, 64]), reads=[pMb], writes=[rden_b])
        K.op("dve", lambda v: v.tensor_tensor(out=cb[0:P, MIX:D].rearrange("p (h d) -> p h d", h=4), in0=pv[:, :, 0:64],
                                              in1=rden[0:P, 0:4].unsqueeze(2).to_broadcast([P, 4, 64]), op=ALU.mult),
             reads=[pMb, rden_b], writes=[cbb])

    def out_proj(c, cb, cbb, pout):
        pt, ptbuf = nextpt()

        def tr(pe):
            for k in range(8):
                ins = pe.transpose(out=pt[:, k, :], in_=cb[:, k * 128:(k + 1) * 128], identity=ident[:])
            return ins
        K.op("pe", tr, reads=[cbb], writes=[ptbuf])
        K.op("act", lambda a: a.activation(out=catT[:].rearrange("p a b -> p (a b)"), in_=pt[:].rearrange("p a b -> p (a b)"), func=AF.Copy),
             reads=[ptbuf], writes=[catT_b])
        for ch in range(2):
            pb, pbb = nextpb()
            K.op("pe", lambda pe, pb=pb, ch=ch: mm_group(
                pe, pb[:], lambda k: catT[:, k, :], lambda k: rv4(pout[ch][k // 4])[:, k % 4, :], 8),
                reads=[catT_b, ring_b[pout[ch][0]], ring_b[pout[ch][1]]], writes=[pbb])
            K.op("dve", lambda v, pb=pb, ch=ch: v.tensor_tensor(out=xres[:, c, ch * 512:(ch + 1) * 512], in0=xres[:, c, ch * 512:(ch + 1) * 512],
                                                                 in1=pb[:], op=ALU.add),
                 reads=[pbb, xres_b[c]], writes=[xres_b[c]])

    def mlp(l, nst=NST, sts=None):
        fence()
        gi = GI["mlp"] + l
        if sts is None:
            sts = list(range(nst))

        def mnormA(st):
            for j in range(CH):
                c = st * CH + j
                norm_A(xres[:, c, :], xres_b[c], j)

        def mnormB(st):
            for j in range(CH):
                norm_B(j, gi, xnall[:, st, :, j * 128:(j + 1) * 128], xnall_b[st][j])
        for hf in range(2):
            pup = [piece_cols(w_up[l], hf * 2048 + i * 256) for i in range(8)]
            pdn = [[piece_rows(w_down[l], hf * 2048 + rg * 512, ch * 512) for rg in range(4)] for ch in range(2)]
            if hf == 0:
                mnormA(sts[0])
                mnormB(sts[0])
            for idx, st in enumerate(sts):
                if hf == 0 and idx + 1 < len(sts):
                    mnormA(sts[idx + 1])
                for f in range(16):
                    pi = pup[f // 2]
                    co = (f % 2) * 128
                    pb, pbb = nextpb()
                    K.op("pe", lambda pe, pb=pb, pi=pi, co=co: mm_group(
                        pe, pb[:, 0:TS], lambda k: rv8(pi)[:, k, co:co + 128], lambda k: xnall[:, st, k, :], 8),
                        reads=[ring_b[pi]] + xnall_b[st], writes=[pbb])
                    hrt, hrb = hrs[f % 2], hrs_b[f % 2]
                    K.op("act", lambda a, pb=pb, hrt=hrt: a.activation(out=hrt[:], in_=pb[:, 0:TS], func=AF.Relu), reads=[pbb], writes=[hrb])
                    K.op("dve", lambda v, f=f, hrt=hrt: v.tensor_tensor(out=hT[:, f, :], in0=hrt[:], in1=hrt[:], op=ALU.mult),
                         reads=[hrb], writes=[hT_b[f]])
                if hf == 0 and idx + 1 < len(sts):
                    mnormB(sts[idx + 1])
                for j in range(CH):
                    c = st * CH + j
                    for ch in range(2):
                        pb, pbb = nextpb()
                        K.op("pe", lambda pe, pb=pb, ch=ch, j=j: mm_group(
                            pe, pb[:], lambda k: hT[:, k, j * 128:(j + 1) * 128], lambda k: rv4(pdn[ch][k // 4])[:, k % 4, :], 16),
                            reads=hT_b + [ring_b[i] for i in pdn[ch]], writes=[pbb])
                        K.op("dve", lambda v, pb=pb, ch=ch, c=c: v.tensor_tensor(
                            out=xres[:, c, ch * 512:(ch + 1) * 512], in0=xres[:, c, ch * 512:(ch + 1) * 512], in1=pb[:], op=ALU.add),
                            reads=[pbb, xres_b[c]], writes=[xres_b[c]])

    KDUP = [(0, 0), (0, 1), (1, 1), (2, 2), (2, 3), (3, 3)]

    def kv_phase(last_group, sts=None):
        p0 = piece_cols(w_kv, 0)
        p1 = piece_cols(w_kv, 256)
        gi = GI["kv"]
        if sts is None:
            sts = list(range(NST))
        pcur = 0
        st_norm_A(sts[0])
        st_norm_B(gi, pcur)
        for idx, st in enumerate(sts):
            set_xn(pcur)
            if idx + 1 < len(sts):
                st_norm_A(sts[idx + 1])
                st_norm_B(gi, 1 - pcur)
            pcur = 1 - pcur
            for bi, (ga, gb) in enumerate(KDUP):
                pb, pbb = nextpb()

                def kf(pe, pb=pb, ga=ga, gb=gb):
                    for half, g in ((0, ga), (1, gb)):
                        for k in range(8):
                            ins = pe.matmul(pb[half * 64:(half + 1) * 64, 0:TS], lhsT=rv8(p0)[:, k, g * 64:(g + 1) * 64], rhs=xnT[:, k, :],
                                            start=(k == 0), stop=(k == 7))
                    return ins
                K.op("pe", kf, reads=[ring_b[p0]] + xnT_b, writes=[pbb])
                K.op("act", lambda a, pb=pb, bi=bi, st=st: a.activation(out=kshT[:, bi, 128 + st * TS:128 + (st + 1) * TS], in_=pb[:, 0:TS], func=AF.Copy),
                     reads=[pbb], writes=[kshT_b[1 + st * CH + jj] for jj in range(CH)])
            for j in range(CH):
                c = st * CH + j
                pb, pbb = nextpb()

                def tm(pe, pb=pb, j=j):
                    mm_group(pe, pb[:, 0:256], lambda k: xnT[:, k, j * 128:(j + 1) * 128], lambda k: rv8(p0)[:, k, :], 8)
                    return mm_group(pe, pb[:, 256:512], lambda k: xnT[:, k, j * 128:(j + 1) * 128], lambda k: rv8(p1)[:, k, :], 8)
                K.op("pe", tm, reads=[ring_b[p0], ring_b[p1], xnT_b[j]], writes=[pbb])
                K.op("dve", lambda v, pb=pb, c=c: v.tensor_copy(out=vsh1[:, 1 + c, :, 0:64], in_=pb[:, 256:512].rearrange("p (h d) -> p h d", h=4)),
                     reads=[pbb], writes=[vsh_b[1 + c]])
                if last_group and c == NCG - 1:
                    K.op("act", lambda a, pb=pb: a.activation(out=kvout[:], in_=pb[:], func=AF.Copy), reads=[pbb], writes=[kvout_b])
                    K.dma("sp", swak_d, kvout[:, 0:256], "misc_out", reads=[kvout_b])
                    K.dma("sp", swav_d, kvout[:, 256:512], "misc_out", reads=[kvout_b])
        set_xn(0)

    def kv_shift():
        K.op("dve", lambda v: v.tensor_copy(out=kshT[:, :, 0:128], in_=kshT[:, :, NCG * 128:(NCG + 1) * 128]),
             reads=[kshT_b[NCG]], writes=[kshT_b[0]])
        K.op("dve", lambda v: v.tensor_copy(out=vsh1[:, 0, :, 0:64], in_=vsh1[:, NCG, :, 0:64]),
             reads=[vsh_b[NCG]], writes=[vsh_b[0]])

    def mixer_b(l, first_group):
        fence()
        jb = l - 2
        wi = w_in_b[jb]
        pin = [piece_cols(wi, c * 256) for c in range(4)]
        wo = w_out_b[jb]
        pout = [[piece_rows(wo, rg * 512, ch * 512) for rg in range(2)] for ch in range(2)]
        gi = GI["mix"] + l
        pcur = 0
        st_norm_A(0)
        st_norm_B(gi, pcur)
        for st in range(NST):
            set_xn(pcur)
            pnext = 1 - pcur
            pcur = pnext
            for blk in range(8):
                col = blk * 128
                pi = pin[col // 256]
                co = col % 256
                pb, pbb = nextpb()
                K.op("pe", lambda pe, pb=pb, pi=pi, co=co: mm_group(
                    pe, pb[:, 0:TS], lambda k: rv8(pi)[:, k, co:co + 128], lambda k: xnT[:, k, :], 8),
                    reads=[ring_b[pi]] + xnT_b, writes=[pbb])
                if blk < 6:
                    K.op("act", lambda a, pb=pb, blk=blk: a.activation(out=qsT[:, blk, :], in_=pb[:, 0:TS], func=AF.Copy),
                         reads=[pbb], writes=[qsT_b])
                else:
                    b2 = blk - 6
                    K.op("act", lambda a, pb=pb, b2=b2: a.activation(out=qmT[:, b2, :], in_=pb[:, 0:TS], func=AF.Copy),
                         reads=[pbb], writes=[qmT_b])
            for hm in range(4):
                pb, pbb = nextpb()
                lo = (hm % 2) * 64

                def sc(pe, pb=pb, hm=hm, lo=lo):
                    for mc in range(2):
                        ins = pe.matmul(pb[:, mc * TS:(mc + 1) * TS], lhsT=mkT[lo:lo + 64, l, hm // 2, mc * 128:(mc + 1) * 128],
                                        rhs=qmT[lo:lo + 64, hm // 2, :], start=True, stop=True)
                    return ins
                K.op("pe", sc, reads=[mk_b, qmT_b], writes=[pbb])
                K.op("act", lambda a, pb=pb, hm=hm: a.activation(out=eT[:, hm, :, :].rearrange("p a b -> p (a b)"), in_=pb[:], func=AF.Exp, scale=0.125),
                     reads=[pbb], writes=[eT_b[hm]])
            if st + 1 < NST:
                st_norm_A(st + 1)
            MB = 99
            for j in range(CH):
                if MB < 1:
                    continue
                c = st * CH + j
                js = slice(j * 128, (j + 1) * 128)
                cb = cat[j]
                cbb = cat_b[j]
                masked_prev = first_group and c == 0
                for hpp in range(3):
                    for hh in range(2):
                        pb, pbb = nextpb()
                        lo = hh * 64

                        def sw(pe, pb=pb, hpp=hpp, lo=lo, js=js, c=c):
                            for i in range(2):
                                hp = 2 * hpp + i
                                for kb in range(2):
                                    kc0 = (c + kb) * 128
                                    ins = pe.matmul(pb[:, (i * 2 + kb) * 128:(i * 2 + kb + 1) * 128],
                                                    lhsT=kshT[lo:lo + 64, hp, kc0:kc0 + 128], rhs=qsT[lo:lo + 64, hp, js], start=True, stop=True)
                            return ins
                        K.op("pe", sw, reads=[kshT_b[c], kshT_b[c + 1], qsT_b], writes=[pbb])
                        stb = st_t if hh == 0 else st_t2
                        stbb = st_b[hh]
                        for i in range(2):
                            hq = (2 * hpp + i) * 2 + hh
                            for kb in range(2):
                                idx = i * 2 + kb
                                dm = dprev if kb == 0 else dcur
                                K.op("dve", lambda v, pb=pb, idx=idx, dm=dm, hq=hq, stb=stb: v.scalar_tensor_tensor(
                                    out=stb[:, idx, :], in0=dm[:], scalar=8.0 * SLOPES[hq], in1=pb[:, idx * 128:(idx + 1) * 128],
                                    op0=ALU.mult, op1=ALU.add),
                                    reads=[pbb], writes=[st_b4[hh][idx]])
                        for i in range(2):
                            hp = 2 * hpp + i
                            hq = hp * 2 + hh
                            if masked_prev:
                                K.op("act", lambda a, i=i, hq=hq, stb=stb: a.activation(out=esT[:, hq, 0, :], in_=stb[:, i * 2, :], func=AF.Exp,
                                                                                       scale=0.125, bias=flags[:, 1:2]),
                                     reads=[st_b4[hh][i * 2]], writes=[esT_b[hp]])
                                K.op("act", lambda a, i=i, hq=hq, stb=stb: a.activation(out=esT[:, hq, 1, :], in_=stb[:, i * 2 + 1, :], func=AF.Exp,
                                                                                       scale=0.125),
                                     reads=[st_b4[hh][i * 2 + 1]], writes=[esT_b[hp]])
                            else:
                                K.op("act", lambda a, i=i, hq=hq, stb=stb: a.activation(out=esT[:, hq, :, :], in_=stb[:, i * 2:i * 2 + 2, :], func=AF.Exp, scale=0.125),
                                     reads=[st_b4[hh][i * 2], st_b4[hh][i * 2 + 1]], writes=[esT_b[hp]])
                if MB < 2:
                    continue
                pOs = [nextpb(), nextpb()]
                for half in range(2):
                    po, pob = pOs[half]

                    def so(pe, po=po, half=half, c=c):
                        for i in range(6):
                            hq = half * 6 + i
                            g = hq // 3
                            for kb in range(2):
                                ins = pe.matmul(po[:, i * 65:(i + 1) * 65], lhsT=esT[:, hq, kb, :], rhs=vsh1[:, c + kb, g, :],
                                                start=(kb == 0), stop=(kb == 1))
                        return ins
                    K.op("pe", so, reads=esT_b[half * 3:half * 3 + 3] + [vsh_b[c], vsh_b[c + 1]], writes=[pob])
                    pv = po[:, 0:390].rearrange("p (h d) -> p h d", h=6)
                    K.op("dve", lambda v, pv=pv, half=half: v.tensor_tensor(out=rden[:, 4:10], in0=pv[:, :, 64],
                                                                           in1=esink[:, jb * 12 + half * 6:jb * 12 + half * 6 + 6], op=ALU.add),
                         reads=[pob], writes=[rden_b])
                    K.op("dve", lambda v: v.reciprocal(out=rden[:, 10:16], in_=rden[:, 4:10]), reads=[rden_b], writes=[rden_b])
                    K.op("dve", lambda v, pv=pv, half=half, cb=cb: v.tensor_tensor(
                        out=cb[:, half * 384:(half + 1) * 384].rearrange("p (h d) -> p h d", h=6), in0=pv[:, :, 0:64],
                        in1=rden[:, 10:16].unsqueeze(2).to_broadcast([128, 6, 64]), op=ALU.mult),
                        reads=[pob, rden_b], writes=[cbb])
                if MB < 3:
                    continue
                pM, pMb = nextpb()

                def mo(pe, pM=pM, js=js):
                    for hm in range(4):
                        for mc in range(2):
                            ins = pe.matmul(pM[:, hm * 65:(hm + 1) * 65], lhsT=eT[:, hm, mc, js], rhs=mv1[:, l, mc, hm, :],
                                            start=(mc == 0), stop=(mc == 1))
                    return ins
                K.op("pe", mo, reads=eT_b + [mv_b], writes=[pMb])
                mem_out(pM, pMb, cb, cbb)
            if st + 1 < NST:
                st_norm_B(gi, pnext)
            for j in range(CH):
                out_proj(st * CH + j, cat[j], cat_b[j], pout)
        set_xn(0)

    def final_out(grp_own):
        for c in range(NCG):
            j = c % 2
            if dbg:
                K.dma("sp", y_d[grp_own * G + c * 128:grp_own * G + (c + 1) * 128, :], xres[:, c, :], f"yst{j}", reads=[xres_b[c]])
                continue
            rstd = rms_stats(xres[:, c, :], xres_b[c], j)
            K.dma("sp", yst[j][:], gfin_d, f"yst{j}", writes=[yst_b[j]])
            K.op("dve", lambda v, c=c, j=j, rstd=rstd: v.scalar_tensor_tensor(out=yst[j][:], in0=xres[:, c, :], scalar=rstd, in1=yst[j][:],
                                                                             op0=ALU.mult, op1=ALU.mult),
                 reads=[xres_b[c], stat_b[j], yst_b[j]], writes=[yst_b[j]])
            K.dma("sp", y_d[grp_own * G + c * 128:grp_own * G + (c + 1) * 128, :], yst[j][:], f"yst{j}", reads=[yst_b[j]])


    GAM = [float(np.exp(LOGG[h])) for h in range(6)]

    def sample_phase():
        es_s = ExitStack()

        def sbs(name, shape, dt):
            return es_s.enter_context(nc.sbuf_tensor("ss_" + name, list(shape), dt))
        eyecol = sbs("eyecol", [128, 16], F32)
        eye16 = sbs("eye16", [128, 16, 16], F32)
        SEL = sbs("SEL", [128, 16, 128], BF16)
        AB = sbs("AB", [128, 12], F32)
        S0 = [sbs(f"S0_{i}", [128, 6, 128], F32) for i in range(2)]
        S0_b = [Buf(f"S0_{i}") for i in range(2)]
        Snb = [sbs(f"Snb{i}", [128, 6, 128], BF16) for i in range(2)]
        Snb_b = [Buf(f"Snb{i}") for i in range(2)]
        km = [sbs(f"km{i}", [128, MIX], BF16) for i in range(2)]
        km_b = [Buf(f"km{i}") for i in range(2)]
        QS = sbs("QS", [128, 16, 6, 16], BF16)
        QS_b = Buf("QS")
        Kc = [sbs(f"Kc{i}", [128, 2, 256], F32) for i in range(2)]
        Kc_b = [Buf(f"Kc{i}") for i in range(2)]
        Vc = [sbs(f"Vc{i}", [128, 2, 256], F32) for i in range(2)]
        Vc_b = [Buf(f"Vc{i}") for i in range(2)]
        V1 = [sbs(f"V1{i}", [128, 2, 4, 65], BF16) for i in range(2)]
        V1_b = [Buf(f"V1{i}") for i in range(2)]
        scs = sbs("scs", [128, 48], F32)
        scs_b = Buf("scs")
        es_t = sbs("es_t", [128, 16], F32)
        es_b = Buf("es_t")
        Esel = [sbs(f"Esel{i}", [128, 12, 16], BF16) for i in range(2)]
        Esel_b = [Buf(f"Esel{i}") for i in range(2)]
        qm_tm = sbs("qm_tm", [128, 256], BF16)
        qm_b = Buf("qm_tm")
        q_tmb = sbs("q_tmb", [128, MIX], BF16)
        q_b = Buf("q_tmb")
        kvn = sbs("kvn", [128, 512], F32)
        kvn_b = Buf("kvn")
        en = sbs("en", [128, 16], F32)
        en_b = Buf("en")
        skeys = []
        for i in range(2):
            for nm in ("S0", "Kc", "Vc"):
                K.newsem(f"{nm}{i}")
                skeys.append(f"{nm}{i}")
        prodf = osq[:].rearrange("p a b -> p (a b)")
        osbf = osb[:].rearrange("p a b -> p (a b)")
        sinit = Buf("sinit")
        K.newsem("cload2")
        K.dma("sp", eyecol[:], eyecol_d, "cload2", writes=[sinit], chain=False)
        K.dma("sp", eye16[:].rearrange("p a b -> p (a b)"), eye16_d, "cload2", writes=[sinit], chain=False)
        K.dma("sp", SEL[:].rearrange("p a b -> p (a b)"), sel_d, "cload2", writes=[sinit], chain=False)
        K.dma("sp", AB[:], ab_d, "cload2", writes=[sinit], chain=False)
        for e in ("pe", "act", "dve"):
            K._wait(e, [sinit], [])
        K.dma("sp", xres[:, 0, :], xs_d, "xload", writes=[xres_b[0]])
        K.op("dve", lambda v: v.memset(xres[:, 1, :], 0.0), writes=[xres_b[1]])
        K.op("dve", lambda v: v.memset(cat[0][:], 0.0), writes=[cat_b[0]])
        for i in range(2):
            K.op("dve", lambda v, i=i: v.memset(V1[i][:].rearrange("p a b c -> p (a b c)"), 1.0), writes=[V1_b[i]])

        def tm_piece(pi, width=256):
            pb, pbb = nextpb()
            K.op("pe", lambda pe: mm_group(pe, pb[:, 0:width], lambda k: xnT[:, k, 0:128], lambda k: rv8(pi)[:, k, 0:width], 8),
                 reads=[ring_b[pi], xnT_b[0]], writes=[pbb])
            return pb, pbb

        def s_mem_attn(l):
            pM, pMb, im = holdpb()
            K.op("dve", lambda v: v.memset(pM[0:16, :], 0.0), writes=[pMb])
            pbs = {}

            def stA(s):
                bf = s % 2
                K.dma("sp", Kc[bf][:], memk_s[l, s].rearrange("(c p) f -> p c f", p=128), f"Kc{bf}", writes=[Kc_b[bf]])
                K.dma("sp", Vc[bf][:], memv_s[l, s].rearrange("(c p) f -> p c f", p=128), f"Vc{bf}", writes=[Vc_b[bf]])
                K.op("act", lambda a: a.activation(out=V1[bf][:, :, :, 0:64], in_=Vc[bf][:].rearrange("p c (h d) -> p c h d", h=4), func=AF.Copy),
                     reads=[Vc_b[bf]], writes=[V1_b[bf]])
                pb, pbb = nextpb()
                K.op("pe", lambda pe: pe.matmul(pb[:, 0:256], lhsT=SEL[:, s, :], rhs=qm_tm[:], start=True, stop=True),
                     reads=[qm_b], writes=[pbb])
                pbs[s] = (pb, pbb)

            def stB(s):
                bf = s % 2
                pb, pbb = pbs.pop(s)
                K.op("dve", lambda v: v.tensor_tensor(out=prodf[:, 0:512].rearrange("p (c f) -> p c f", c=2), in0=Kc[bf][:],
                                                      in1=pb[:, 0:256].unsqueeze(1).to_broadcast([128, 2, 256]), op=ALU.mult),
                     reads=[pbb, Kc_b[bf]], writes=[osq_b])
                K.op("dve", lambda v: v.tensor_reduce(out=scs[:, 0:8], in_=prodf[:, 0:512].rearrange("p (g d) -> p g d", d=64), axis=AX.X, op=ALU.add),
                     reads=[osq_b], writes=[scs_b])
                K.op("act", lambda a: a.activation(out=es_t[:, 0:8], in_=scs[:, 0:8], func=AF.Exp, scale=0.125), reads=[scs_b], writes=[es_b])
                K.op("dve", lambda v: v.tensor_tensor(out=Esel[bf][:, 0:8, :], in0=es_t[:, 0:8].unsqueeze(2).to_broadcast([128, 8, 16]),
                                                      in1=eye16[:, s, :].unsqueeze(1).to_broadcast([128, 8, 16]), op=ALU.mult),
                     reads=[es_b], writes=[Esel_b[bf]])

            def stC(s):
                bf = s % 2

                def pv(pe):
                    for h in range(4):
                        for mc in range(2):
                            ins = pe.matmul(pM[0:16, h * 65:(h + 1) * 65], lhsT=Esel[bf][:, mc * 4 + h, :], rhs=V1[bf][:, mc, h, :],
                                            start=False, stop=(s == 15 and mc == 1), skip_group_check=True)
                    return ins
                K.op("pe", pv, reads=[Esel_b[bf], V1_b[bf]], writes=[pMb] if s in (0, 15) else [])
            stA(0)
            for s in range(16):
                if s + 1 < 16:
                    stA(s + 1)
                stB(s)
                stC(s)
            mem_out(pM, pMb, cat[0], cat_b[0], P=16)
            held.discard(im)

        def s_mixer_a(l):
            fence()
            pin = [piece_cols(w_in_a[l], c * 256) for c in range(13)]
            pout = [[piece_rows(w_out_a[l], rg * 512, ch * 512) for rg in range(2)] for ch in range(2)]
            norm_T(xres[:, 0, :], xres_b[0], 0, GI["mix"] + l, xnT[:, :, 0:128], xnT_b[0])
            for h in range(6):
                col = h * 128
                pi = pin[col // 256]
                co = col % 256
                pb, pbb = nextpb()
                K.op("pe", lambda pe, pb=pb, pi=pi, co=co: mm_group(pe, pb[:, 0:128], lambda k: rv8(pi)[:, k, co:co + 128], lambda k: xnT[:, k, 0:128], 8),
                     reads=[ring_b[pi], xnT_b[0]], writes=[pbb])
                K.op("dve", lambda v, pb=pb, h=h: v.tensor_tensor(out=QS[:, :, h, :], in0=pb[:, 0:16].unsqueeze(1).to_broadcast([128, 16, 16]),
                                                                   in1=eye16[:], op=ALU.mult),
                     reads=[pbb], writes=[QS_b])
            for pp in range(3, 13):
                pb, pbb = tm_piece(pin[pp])
                cc = (pp - 3) % 3 * 256
                if pp < 6:
                    K.op("act", lambda a, pb=pb, cc=cc: a.activation(out=ktm[0][:, cc:cc + 256], in_=pb[:, 0:256], func=AF.Copy), reads=[pbb], writes=[ktm_b[0]])
                elif pp < 9:
                    K.op("act", lambda a, pb=pb, cc=cc: a.activation(out=vtm[0][:, cc:cc + 256], in_=pb[:, 0:256], func=AF.Copy), reads=[pbb], writes=[vtm_b[0]])
                elif pp < 12:
                    K.op("act", lambda a, pb=pb, cc=cc: a.activation(out=gtm[0][:, cc:cc + 256], in_=pb[:, 0:256], func=AF.Silu), reads=[pbb], writes=[gtm_b[0]])
                else:
                    K.op("act", lambda a, pb=pb: a.activation(out=qm_tm[:], in_=pb[:, 0:256], func=AF.Copy), reads=[pbb], writes=[qm_b])
            pOa, pOab, ia = holdpb()
            pOb, pObb, ib = holdpb()
            K.op("dve", lambda v: v.memset(pOa[0:16, :], 0.0), writes=[pOab])
            K.op("dve", lambda v: v.memset(pOb[0:16, :], 0.0), writes=[pObb])
            pks = {}

            def rA(s):
                bf = s % 2
                K.dma("sp", S0[bf][:], state_s[l, s].rearrange("h d e -> d h e"), f"S0{bf}", writes=[S0_b[bf]])
                K.op("dve", lambda v: v.tensor_scalar(out=km[bf][:], in0=ktm[0][:, :], scalar1=eyecol[:, s:s + 1], scalar2=None, op0=ALU.mult),
                     reads=[ktm_b[0]], writes=[km_b[bf]])
                pKa, pKab = nextpb()
                pKb, pKbb = nextpb()

                def kvm(pe, pk, hs):
                    for i, h in enumerate(hs):
                        ins = pe.matmul(pk[:, i * 128:(i + 1) * 128], lhsT=km[bf][:, h * 128:(h + 1) * 128], rhs=vtm[0][:, h * 128:(h + 1) * 128], start=True, stop=True)
                    return ins
                K.op("pe", lambda pe: kvm(pe, pKa, (0, 1, 2, 3)), reads=[km_b[bf], vtm_b[0]], writes=[pKab])
                K.op("pe", lambda pe: kvm(pe, pKb, (4, 5)), reads=[km_b[bf], vtm_b[0]], writes=[pKbb])
                pks[s] = (pKa, pKab, pKb, pKbb)

            def rB(s):
                bf = s % 2
                pKa, pKab, pKb, pKbb = pks.pop(s)
                for h in range(6):
                    pk, pkb, i = (pKa, pKab, h) if h < 4 else (pKb, pKbb, h - 4)
                    K.op("dve", lambda v, pk=pk, i=i, h=h: v.scalar_tensor_tensor(
                        out=S0[bf][:, h, :], in0=S0[bf][:, h, :], scalar=GAM[h], in1=pk[:, i * 128:(i + 1) * 128], op0=ALU.mult, op1=ALU.add),
                        reads=[pkb, S0_b[bf]], writes=[S0_b[bf]])
                K.dma("sp", rets_d[l, s].rearrange("h d e -> d h e"), S0[bf][:], f"S0{bf}", reads=[S0_b[bf]])
                K.op("act", lambda a: a.activation(out=Snb[bf][:].rearrange("p a b -> p (a b)"), in_=S0[bf][:].rearrange("p a b -> p (a b)"), func=AF.Copy),
                     reads=[S0_b[bf]], writes=[Snb_b[bf]])

            def rC(s):
                bf = s % 2

                def om(pe):
                    for h in range(6):
                        po, i = (pOa, h) if h < 4 else (pOb, h - 4)
                        ins = pe.matmul(po[0:16, i * 128:(i + 1) * 128], lhsT=QS[:, s, h, :], rhs=Snb[bf][:, h, :], start=False, stop=(s == 15), skip_group_check=True)
                    return ins
                K.op("pe", om, reads=[QS_b, Snb_b[bf]], writes=[pOab, pObb] if s in (0, 15) else [])
            rA(0)
            for s in range(16):
                if s + 1 < 16:
                    rA(s + 1)
                rB(s)
                rC(s)
            gn_gate(pOa, pOab, pOb, pObb, cat[0], cat_b[0], gtm[0], gtm_b[0], 16)
            held.discard(ia)
            held.discard(ib)
            s_mem_attn(l)
            out_proj(0, cat[0], cat_b[0], pout)

        def s_kv_phase():
            p0 = piece_cols(w_kv, 0)
            p1 = piece_cols(w_kv, 256)
            norm_T(xres[:, 0, :], xres_b[0], 0, GI["kv"], xnT[:, :, 0:128], xnT_b[0])
            pb, pbb = nextpb()

            def tm(pe):
                mm_group(pe, pb[:, 0:256], lambda k: xnT[:, k, 0:128], lambda k: rv8(p0)[:, k, :], 8)
                return mm_group(pe, pb[:, 256:512], lambda k: xnT[:, k, 0:128], lambda k: rv8(p1)[:, k, :], 8)
            K.op("pe", tm, reads=[ring_b[p0], ring_b[p1], xnT_b[0]], writes=[pbb])
            K.op("act", lambda a: a.activation(out=kvn[:], in_=pb[:], func=AF.Copy), reads=[pbb], writes=[kvn_b])
            K.dma("sp", swako_d[:, 127, :], kvn[0:16, 0:256], "misc_out", reads=[kvn_b])
            K.dma("sp", swavo_d[:, 127, :], kvn[0:16, 256:512], "misc_out", reads=[kvn_b])
            K.dma("sp", swako_d[:, 0:127, :], swak_s[:, 1:128, :], "misc_out")
            K.dma("sp", swavo_d[:, 0:127, :], swav_s[:, 1:128, :], "misc_out")

        def s_mixer_b(l):
            fence()
            jb = l - 2
            pin = [piece_cols(w_in_b[jb], c * 256) for c in range(4)]
            pout = [[piece_rows(w_out_b[jb], rg * 512, ch * 512) for rg in range(2)] for ch in range(2)]
            norm_T(xres[:, 0, :], xres_b[0], 0, GI["mix"] + l, xnT[:, :, 0:128], xnT_b[0])
            for pp in range(4):
                pb, pbb = tm_piece(pin[pp])
                if pp < 3:
                    K.op("act", lambda a, pb=pb, pp=pp: a.activation(out=q_tmb[:, pp * 256:(pp + 1) * 256], in_=pb[:, 0:256], func=AF.Copy), reads=[pbb], writes=[q_b])
                else:
                    K.op("act", lambda a, pb=pb: a.activation(out=qm_tm[:], in_=pb[:, 0:256], func=AF.Copy), reads=[pbb], writes=[qm_b])
            K.op("dve", lambda v: v.tensor_tensor(out=prodf[0:16, :].rearrange("p (g r d) -> p g r d", g=4, r=3),
                                                  in0=q_tmb[0:16, :].rearrange("p (g r d) -> p g r d", g=4, r=3),
                                                  in1=kvn[0:16, 0:256].rearrange("p (g d) -> p g d", g=4).unsqueeze(2).to_broadcast([16, 4, 3, 64]), op=ALU.mult),
                 reads=[q_b, kvn_b], writes=[osq_b])
            K.op("dve", lambda v: v.tensor_reduce(out=scs[0:16, 16:28], in_=prodf[0:16, :].rearrange("p (g d) -> p g d", d=64), axis=AX.X, op=ALU.add),
                 reads=[osq_b], writes=[scs_b])
            K.op("act", lambda a: a.activation(out=en[0:16, 0:12], in_=scs[0:16, 16:28], func=AF.Exp, scale=0.125), reads=[scs_b], writes=[en_b])
            pSa, pSab, ia = holdpb()
            pSb, pSbb, ib = holdpb()
            K.op("dve", lambda v: v.memset(pSa[0:16, :], 0.0), writes=[pSab])
            K.op("dve", lambda v: v.memset(pSb[0:16, :], 0.0), writes=[pSbb])
            qbs = {}

            def wA(s):
                bf = s % 2
                K.dma("sp", Kc[bf][:, 0, :], swak_s[s], f"Kc{bf}", writes=[Kc_b[bf]])
                K.dma("sp", Vc[bf][:, 0, :], swav_s[s], f"Vc{bf}", writes=[Vc_b[bf]])
                K.op("act", lambda a: a.activation(out=V1[bf][:, 0, :, 0:64], in_=Vc[bf][:, 0, :].rearrange("p (h d) -> p h d", h=4), func=AF.Copy),
                     reads=[Vc_b[bf]], writes=[V1_b[bf]])
                lst = []
                for half in range(2):
                    pb, pbb = nextpb()
                    K.op("pe", lambda pe, pb=pb, half=half: pe.matmul(pb[:, 0:384], lhsT=SEL[:, s, :], rhs=q_tmb[:, half * 384:(half + 1) * 384], start=True, stop=True),
                         reads=[q_b], writes=[pbb])
                    lst.append((pb, pbb))
                qbs[s] = lst

            def wB(s):
                bf = s % 2
                lst = qbs.pop(s)
                for half in range(2):
                    pb, pbb = lst[half]
                    K.op("dve", lambda v, pb=pb, half=half: v.tensor_tensor(
                        out=prodf[:, half * 384:(half + 1) * 384].rearrange("p (g r d) -> p g r d", g=2, r=3),
                        in0=Kc[bf][:, 0, half * 128:(half + 1) * 128].rearrange("p (g d) -> p g d", g=2).unsqueeze(2).to_broadcast([128, 2, 3, 64]),
                        in1=pb[:, 0:384].rearrange("p (g r d) -> p g r d", g=2, r=3), op=ALU.mult),
                        reads=[pbb, Kc_b[bf]], writes=[osq_b])
                K.op("dve", lambda v: v.tensor_reduce(out=scs[:, 0:12], in_=prodf[:, :].rearrange("p (g d) -> p g d", d=64), axis=AX.X, op=ALU.add),
                     reads=[osq_b], writes=[scs_b])
                K.op("dve", lambda v: v.scalar_tensor_tensor(out=scs[:, 32:44], in0=scs[:, 0:12], scalar=0.125, in1=AB[:], op0=ALU.mult, op1=ALU.add),
                     reads=[scs_b], writes=[scs_b])
                K.op("act", lambda a: a.activation(out=es_t[:, 0:12], in_=scs[:, 32:44], func=AF.Exp), reads=[scs_b], writes=[es_b])
                K.op("dve", lambda v: v.tensor_tensor(out=Esel[bf][:, 0:12, :], in0=es_t[:, 0:12].unsqueeze(2).to_broadcast([128, 12, 16]),
                                                      in1=eye16[:, s, :].unsqueeze(1).to_broadcast([128, 12, 16]), op=ALU.mult),
                     reads=[es_b], writes=[Esel_b[bf]])

            def wC(s):
                bf = s % 2

                def pv(pe):
                    for hq in range(12):
                        po = pSa if hq < 6 else pSb
                        i = hq % 6
                        ins = pe.matmul(po[0:16, i * 65:(i + 1) * 65], lhsT=Esel[bf][:, hq, :], rhs=V1[bf][:, 0, hq // 3, :], start=False, stop=(s == 15), skip_group_check=True)
                    return ins
                K.op("pe", pv, reads=[Esel_b[bf], V1_b[bf]], writes=[pSab, pSbb] if s in (0, 15) else [])
            wA(0)
            for s in range(16):
                if s + 1 < 16:
                    wA(s + 1)
                wB(s)
                wC(s)
            for half in range(2):
                po, pob = (pSa, pSab) if half == 0 else (pSb, pSbb)
                pvw = po[0:16, 0:390].rearrange("p (h d) -> p h d", h=6)
                t1 = osbf[0:16, 0:384].rearrange("p (g r d) -> p g r d", g=2, r=3)
                K.op("dve", lambda v, half=half, t1=t1: v.tensor_tensor(
                    out=t1, in0=kvn[0:16, 256 + half * 128:256 + (half + 1) * 128].rearrange("p (g d) -> p g d", g=2).unsqueeze(2).to_broadcast([16, 2, 3, 64]),
                    in1=en[0:16, half * 6:(half + 1) * 6].rearrange("p (g r) -> p g r", g=2).unsqueeze(3).to_broadcast([16, 2, 3, 64]), op=ALU.mult),
                    reads=[kvn_b, en_b], writes=[osb_b])
                t1f = osbf[0:16, 0:384].rearrange("p (h d) -> p h d", h=6)
                K.op("dve", lambda v, t1f=t1f, pvw=pvw: v.tensor_tensor(out=t1f, in0=t1f, in1=pvw[:, :, 0:64], op=ALU.add),
                     reads=[pob, osb_b], writes=[osb_b])
                K.op("dve", lambda v, pvw=pvw, half=half: v.tensor_tensor(out=rden[0:16, 4:10], in0=pvw[:, :, 64], in1=en[0:16, half * 6:(half + 1) * 6], op=ALU.add),
                     reads=[pob, en_b], writes=[rden_b])
                K.op("dve", lambda v, half=half: v.tensor_tensor(out=rden[0:16, 4:10], in0=rden[0:16, 4:10],
                                                                 in1=esink[0:16, jb * 12 + half * 6:jb * 12 + half * 6 + 6], op=ALU.add),
                     reads=[rden_b], writes=[rden_b])
                K.op("dve", lambda v: v.reciprocal(out=rden[0:16, 10:16], in_=rden[0:16, 4:10]), reads=[rden_b], writes=[rden_b])
                K.op("dve", lambda v, half=half, t1f=t1f: v.tensor_tensor(
                    out=cat[0][0:16, half * 384:(half + 1) * 384].rearrange("p (h d) -> p h d", h=6), in0=t1f,
                    in1=rden[0:16, 10:16].unsqueeze(2).to_broadcast([16, 6, 64]), op=ALU.mult),
                    reads=[osb_b, rden_b], writes=[cat_b[0]])
            held.discard(ia)
            held.discard(ib)
            s_mem_attn(l)
            out_proj(0, cat[0], cat_b[0], pout)

        for l in range(min(n_layers, 4)):
            if l < 2:
                s_mixer_a(l)
            else:
                s_mixer_b(l)
            mlp(l, nst=1)
            if l == 1:
                s_kv_phase()
        if dbg:
            K.dma("sp", ys_d, xres[0:16, 0, :], "yst0", reads=[xres_b[0]])
        else:
            rstd = rms_stats(xres[:, 0, :], xres_b[0], 0)
            K.dma("sp", yst0[:], gfin_d, "yst0", writes=[yb0])
            K.op("dve", lambda v: v.scalar_tensor_tensor(out=yst0[:], in0=xres[:, 0, :], scalar=rstd, in1=yst0[:], op0=ALU.mult, op1=ALU.mult),
                 reads=[xres_b[0], stat_b[0], yb0], writes=[yb0])
            K.dma("sp", ys_d, yst0[0:16, :], "yst0", reads=[yb0])
        keys = skeys + ["yst0", "misc_out", "xload", "cload", "cload2"]
        for e in ("pe", "act", "dve", "sp"):
            for f in ("pe", "act", "dve"):
                if f != e:
                    kf = K.cur[f]
                    if K.cnt[kf] > 0 and K.known[e].get(kf, 0) < K.cnt[kf]:
                        K.eng[e].wait_ge(K.sems[kf], K.cnt[kf])
                        K.known[e][kf] = K.cnt[kf]
            for kk in keys:
                if K.cnt[kk] > 0 and K.known[e].get(kk, 0) < K.cnt[kk]:
                    K.eng[e].wait_ge(K.sems[kk], K.cnt[kk])
                    K.known[e][kk] = K.cnt[kk]
        es_s.close()

    STAGE = 99
    if True:
        sample_phase()
    S = [sb(f"S{l}", [128, 6, 128], F32) for l in range(2)]
    Sb = [[sb(f"Sb{l}_{p}", [128, 6, 128], BF16) for p in range(2)] for l in range(2)]
    mkT = sb("mkT", [128, 4, 2, 256], BF16)
    mv1 = sb("mv1", [128, 4, 2, 4, 65], BF16)
    kshT = sb("kshT", [128, 6, (NCG + 1) * 128], BF16)
    vsh1 = sb("vsh1", [128, NCG + 1, 4, 65], BF16)
    K.op("dve", lambda v: v.memset(mv1[:].rearrange("p a b c d -> p (a b c d)"), 1.0), writes=[mv_b])
    K.op("dve", lambda v: v.memset(vsh1[:].rearrange("p a b c -> p (a b c)"), 1.0), writes=vsh_b)
    K.op("dve", lambda v: v.memset(kshT[:].rearrange("p a b -> p (a b)"), 0.0), writes=kshT_b)
    for l in range(2):
        K.op("dve", lambda v, l=l: v.memset(S[l][:].rearrange("p a b -> p (a b)"), 0.0), writes=S_b[l])
        for p_ in range(2):
            K.op("dve", lambda v, l=l, p_=p_: v.memset(Sb[l][p_][:].rearrange("p a b -> p (a b)"), 0.0), writes=Sb_b[l][p_])
    K.dma("sp", memx_sb[:], memx.rearrange("(c p) d -> p c d", p=128), "xload", writes=[xres_b[0], xres_b[1]])
    if STAGE >= 0:
        mem_prologue()
    for grp in range(4):
        if STAGE < 1:
            break
        isP = grp < 2
        src = xp if isP else xo
        g0 = (grp % 2) * G
        K.dma("sp", xres[:], src[g0:g0 + G, :].rearrange("(c p) d -> p c d", p=128), "xload", writes=xres_b)
        if grp == 2:
            for l in range(2):
                K.op("dve", lambda v, l=l: v.tensor_scalar(out=S[l][:].rearrange("p a b -> p (a b)"), in0=S[l][:].rearrange("p a b -> p (a b)"),
                                                          scalar1=flags[:, 0:1], scalar2=None, op0=ALU.mult),
                     reads=S_b[l], writes=S_b[l])
                K.op("act", lambda a, l=l: a.activation(out=Sb[l][0][:].rearrange("p a b -> p (a b)"), in_=S[l][:].rearrange("p a b -> p (a b)"), func=AF.Copy),
                     reads=S_b[l], writes=Sb_b[l][0])
        layers = [0, 1] if isP else [0, 1, 2, 3]
        layers = [l for l in layers if l < n_layers]
        for l in layers:
            if STAGE < 2:
                break
            if isP and l == 1:
                if grp == 0:
                    mixer_a(1, full_sts=set())
                else:
                    mixer_a(1, full_sts={NST - 1})
                    mlp(1, sts=[NST - 1])
                    kv_phase(last_group=False, sts=[NST - 1])
                continue
            if l < 2:
                mixer_a(l)
            else:
                mixer_b(l, first_group=(grp == 2))
            if STAGE < 3:
                continue
            mlp(l)
            if l == 1:
                kv_phase(last_group=(grp == 3))
        if 1 in layers and STAGE >= 3 and grp >= 1:
            kv_shift()
        if not isP:
            final_out(grp - 2)
    for l in range(2):
        K.dma("sp", ret_d[l].rearrange("h d e -> d h e"), S[l][:], "misc_out", reads=S_b[l])
    fin = {}
    for key in ["misc_out", "yst0", "yst1"]:
        if K.cnt[key] > 0:
            nc.sync.wait_ge(K.sems[key], K.cnt[key])
    for e in ("pe", "act", "dve"):
        key = K.cur[e]
        if K.cnt[key] > 0:
            nc.sync.wait_ge(K.sems[key], K.cnt[key])
    for i in range(NSLOT):
        if K.cnt[f"ring{i}"] > 0:
            nc.sync.wait_ge(K.sems[f"ring{i}"], K.cnt[f"ring{i}"])
    print("ops:", K.nops, len(K.sems))


def host_consts():
    c = {}
    c["ident"] = np.eye(128, dtype=np.float32).astype(ml_dtypes.bfloat16)
    q = np.arange(128)
    maskT = np.zeros((128, 6, 128), np.float32)
    wq = np.zeros((128, 6, 128), np.float32)
    wk = np.zeros((128, 6), np.float32)
    for h in range(6):
        diff = q[None, :] - q[:, None]
        m = np.where(diff >= 0, np.exp(np.maximum(diff, 0) * LOGG[h]), 0.0) * (128.0 ** -0.5)
        maskT[:, h, :] = m
        wq[:, h, :] = np.exp((q + 1.0) * LOGG[h])[None, :]
        wk[:, h] = np.exp((127.0 - q) * LOGG[h]) * (128.0 ** -0.5)
    c["maskT"] = maskT.reshape(128, 768)
    c["wq"] = wq.reshape(128, 768)
    c["wk"] = wk
    kk = q[:, None].astype(np.float32)
    qq = q[None, :].astype(np.float32)
    NEG = -1.0e7
    c["dcur"] = np.where(qq >= kk, kk - qq, NEG).astype(np.float32).astype(ml_dtypes.bfloat16)
    c["dprev"] = np.where(kk >= qq, kk - qq - 128.0, NEG).astype(np.float32).astype(ml_dtypes.bfloat16)
    eyecol = np.zeros((128, 16), np.float32)
    eye16 = np.zeros((128, 16, 16), np.float32)
    sel = np.zeros((128, 16, 128), np.float32)
    for s_ in range(16):
        eyecol[s_, s_] = 128.0 ** -0.5
        eye16[:, s_, s_] = 1.0
        sel[s_, s_, :] = 1.0
    c["eyecol"] = eyecol
    c["eye16"] = eye16.reshape(128, 256)
    c["sel"] = sel.reshape(128, 16 * 128).astype(ml_dtypes.bfloat16)
    ab = np.zeros((128, 12), np.float32)
    for hq in range(12):
        ab[:, hq] = -SLOPES[hq] * (128.0 - q)
    c["ab"] = ab
    return c


def kernel(x_prompt, x_sample, cache_mem_k, cache_mem_v, state_ret, cache_swa_k, cache_swa_v, mem_prompt,
           norm_mix, w_in_a, w_out_a, w_in_b, w_out_b, attn_sinks, norm_mem, w_mem_kv, norm_kv, w_kv,
           norm_mlp, w_up, w_down, norm_final, _n_layers=4, _dbg=False, _ncores=8):
    f = lambda a: np.ascontiguousarray(np.asarray(a, dtype=np.float32))
    x_prompt = f(x_prompt)
    consts = host_consts()
    gvecs = np.concatenate([f(norm_mix), f(norm_mlp), f(norm_mem), f(norm_kv)[None], f(norm_final)[None]], axis=0)
    gains = np.ascontiguousarray(gvecs.reshape(14, 8, 128).transpose(2, 0, 1)).reshape(128, 14 * 8)
    gfin = np.ascontiguousarray(np.broadcast_to(f(norm_final)[None, :], (128, D)))
    sinks = np.ascontiguousarray(np.broadcast_to(f(attn_sinks).reshape(1, 24), (128, 24)))
    shared = dict(
        w_in_a=f(w_in_a), w_out_a=f(w_out_a), w_in_b=f(w_in_b), w_out_b=f(w_out_b), w_mem_kv=f(w_mem_kv), w_kv=f(w_kv),
        w_up=f(w_up), w_down=f(w_down), gains=gains, gfin=gfin, sinks=sinks, **consts)
    in_maps = []
    for c in range(8):
        b, half = c // 2, c % 2
        flags = np.zeros((128, 2), np.float32)
        flags[:, 0] = 1.0 if half == 1 else 0.0
        flags[:, 1] = 0.0 if half == 1 else -1.0e9
        m = dict(shared)
        m["xp"] = np.ascontiguousarray(x_prompt[b, 0:HALF])
        m["xo"] = np.ascontiguousarray(x_prompt[b, half * HALF:(half + 1) * HALF])
        m["memx"] = f(mem_prompt)[b]
        m["flags"] = flags
        xs = np.zeros((128, D), np.float32)
        xs[0:16] = f(x_sample)[16 * c:16 * c + 16, 0, :]
        m["xs"] = xs
        m["state_s"] = np.ascontiguousarray(f(state_ret)[:, 16 * c:16 * c + 16])
        m["memk_s"] = np.ascontiguousarray(f(cache_mem_k)[:, 16 * c:16 * c + 16]).reshape(4, 16, 256, 256)
        m["memv_s"] = np.ascontiguousarray(f(cache_mem_v)[:, 16 * c:16 * c + 16]).reshape(4, 16, 256, 256)
        m["swak_s"] = np.ascontiguousarray(f(cache_swa_k)[16 * c:16 * c + 16]).reshape(16, 128, 256)
        m["swav_s"] = np.ascontiguousarray(f(cache_swa_v)[16 * c:16 * c + 16]).reshape(16, 128, 256)
        in_maps.append(m)
    nc = build(_n_layers, _dbg)
    if _ncores < 8:
        res = run_bass_kernel_spmd(nc, in_maps[:_ncores], core_ids=list(range(_ncores)))
        R = list(res.results)
        while len(R) < 8:
            R.append(R[len(R) % _ncores])
    else:
        res = run_bass_kernel_spmd(nc, in_maps, core_ids=list(range(8)))
        R = res.results
    y_prompt = np.stack([np.concatenate([R[2 * b]["y"], R[2 * b + 1]["y"]], axis=0) for b in range(4)])
    ret_prompt = np.stack([R[2 * b + 1]["ret"] for b in range(4)], axis=1)
    swa_k = np.stack([R[2 * b + 1]["swak"].reshape(128, 4, 64) for b in range(4)])
    swa_v = np.stack([R[2 * b + 1]["swav"].reshape(128, 4, 64) for b in range(4)])
    mem_k = np.stack([R[2 * b]["memk"].reshape(4, 256, 4, 64) for b in range(4)], axis=1)
    mem_v = np.stack([R[2 * b]["memv"].reshape(4, 256, 4, 64) for b in range(4)], axis=1)
    y_sample = np.concatenate([R[c]["ys"] for c in range(8)], axis=0).reshape(128, 1, D)
    ret_sample = np.concatenate([R[c]["rets"] for c in range(8)], axis=1)
    swa_ks = np.concatenate([R[c]["swako"] for c in range(8)], axis=0).reshape(128, 128, 4, 64)
    swa_vs = np.concatenate([R[c]["swavo"] for c in range(8)], axis=0).reshape(128, 128, 4, 64)
    return (y_prompt, y_sample, ret_prompt, ret_sample, swa_k, swa_v, swa_ks, swa_vs, mem_k, mem_v)
```
